# Optimizing a Trainium2 kernel written in Bass

```python
import math
import jax, jax.numpy as jnp
from jax import lax
import numpy as np

D_MODEL = 1024
BATCH = 2
SEQ = 8192
DEPTH = 1

PLE_DIM = 256
N_HEADS = 4
NOPE_DIM = 128
ROPE_DIM = 64
V_DIM = 128
QK_DIM = NOPE_DIM + ROPE_DIM
Q_LORA = 256
KV_LORA = 128
ATTN_WIDTH = N_HEADS * V_DIM
CONV_WIDTH = D_MODEL - ATTN_WIDTH
CONV_K = 3
ROPE_THETA = 10000.0
RMS_EPS = 1e-6
Q_BLOCK = 128
NEG_INF = -1e30
IN_WIDTHS = (Q_LORA, KV_LORA, ROPE_DIM, ATTN_WIDTH,
             CONV_WIDTH, CONV_WIDTH, CONV_WIDTH, CONV_WIDTH)
IN_TOTAL = Q_LORA + KV_LORA + ROPE_DIM + ATTN_WIDTH + 4 * CONV_WIDTH

kernel_name = "hymba_mla_shortconv_ple_block"


def rms_norm(x, g):
    xf = x.astype(jnp.float32)
    y = xf * lax.rsqrt(jnp.mean(xf * xf, axis=-1, keepdims=True) + RMS_EPS)
    return (y * g.astype(jnp.float32)).astype(x.dtype)


def rope_cos_sin(positions):
    inv_freq = 1.0 / (ROPE_THETA ** (jnp.arange(0, ROPE_DIM, 2, dtype=jnp.float32) / ROPE_DIM))
    ang = positions.astype(jnp.float32)[..., None] * inv_freq
    return jnp.cos(ang)[:, :, None, :], jnp.sin(ang)[:, :, None, :]


def apply_rope(t, cos, sin):
    tf = t.astype(jnp.float32)
    t1, t2 = tf[..., : ROPE_DIM // 2], tf[..., ROPE_DIM // 2:]
    out = jnp.concatenate([t1 * cos - t2 * sin, t2 * cos + t1 * sin], axis=-1)
    return out.astype(t.dtype)


def causal_block_attention(q, k, v):
    B, S, H, D = q.shape
    nb = S // Q_BLOCK
    scale = 1.0 / math.sqrt(D)
    q_blocks = q.reshape(B, nb, Q_BLOCK, H, D).transpose(1, 0, 2, 3, 4)
    k_pos = jnp.arange(S)

    def one_block(args):
        qb, bi = args
        s = jnp.einsum('bqhd,bkhd->bhqk', qb, k, preferred_element_type=jnp.float32) * scale
        q_pos = bi * Q_BLOCK + jnp.arange(Q_BLOCK)
        mask = k_pos[None, :] <= q_pos[:, None]
        s = jnp.where(mask[None, None], s, NEG_INF)
        pr = jax.nn.softmax(s, axis=-1).astype(v.dtype)
        return jnp.einsum('bhqk,bkhd->bqhd', pr, v)

    out = lax.map(one_block, (q_blocks, jnp.arange(nb)))
    return out.transpose(1, 0, 2, 3, 4).reshape(B, S, H, v.shape[-1])


def causal_depthwise_conv(u, w):
    C = u.shape[-1]
    return lax.conv_general_dilated(
        u, w[:, None, :].astype(u.dtype), window_strides=(1,),
        padding=[(CONV_K - 1, 0)], dimension_numbers=('NWC', 'WIO', 'NWC'),
        feature_group_count=C)


def hybrid_layer(x, p_i, cos, sin, g_in, w_in, g_cq, w_uq, g_ckv, w_ukv, g_q, g_k,
                 conv_w, g_oa, g_oc, w_o, w_pl, w_plg, g_pl):
    B, S, _ = x.shape
    h = rms_norm(x, g_in)
    proj = h @ w_in
    splits, acc = [], 0
    for wdt in IN_WIDTHS[:-1]:
        acc += wdt
        splits.append(acc)
    c_q, c_kv, k_pe, z_a, cb, cc, cx, z_c = jnp.split(proj, splits, axis=-1)

    q = (rms_norm(c_q, g_cq) @ w_uq).reshape(B, S, N_HEADS, QK_DIM)
    kv = (rms_norm(c_kv, g_ckv) @ w_ukv).reshape(B, S, N_HEADS, NOPE_DIM + V_DIM)
    k_nope, v = kv[..., :NOPE_DIM], kv[..., NOPE_DIM:]
    k = jnp.concatenate(
        [k_nope, jnp.broadcast_to(k_pe[:, :, None, :], (B, S, N_HEADS, ROPE_DIM))], axis=-1)
    q = rms_norm(q, g_q)
    k = rms_norm(k, g_k)
    q = jnp.concatenate([q[..., :NOPE_DIM], apply_rope(q[..., NOPE_DIM:], cos, sin)], axis=-1)
    k = jnp.concatenate([k[..., :NOPE_DIM], apply_rope(k[..., NOPE_DIM:], cos, sin)], axis=-1)
    o_attn = causal_block_attention(q, k, v).reshape(B, S, ATTN_WIDTH)
    y_attn = rms_norm(o_attn * jax.nn.silu(z_a), g_oa)

    u = causal_depthwise_conv(cc * cx, conv_w)
    y_conv = rms_norm(cb * u * jax.nn.silu(z_c), g_oc)

    x = x + jnp.concatenate([y_attn, y_conv], axis=-1) @ w_o

    gate = jax.nn.sigmoid(rms_norm(x, g_pl) @ w_plg)
    return x + gate * (p_i @ w_pl)


def setup_inputs(seed: int = 0) -> dict:
    key = jax.random.key(seed)
    ks = jax.random.split(key, 20)
    f32 = jnp.float32

    def nrm(k, shape, fan_in):
        return jax.random.normal(k, shape, f32) * (fan_in ** -0.5)

    def gain(k, n):
        return 1.0 + 0.02 * jax.random.normal(k, (DEPTH, n), f32)

    x = jax.random.normal(ks[0], (BATCH, SEQ, D_MODEL), f32)
    p = jax.random.normal(ks[1], (DEPTH, BATCH, SEQ, PLE_DIM), f32)
    positions = jnp.broadcast_to(jnp.arange(SEQ, dtype=jnp.int32)[None, :], (BATCH, SEQ))
    return {
        "x": x,
        "p": p,
        "positions": positions,
        "g_in": gain(ks[2], D_MODEL),
        "w_in": nrm(ks[3], (DEPTH, D_MODEL, IN_TOTAL), D_MODEL),
        "g_cq": gain(ks[4], Q_LORA),
        "w_uq": nrm(ks[5], (DEPTH, Q_LORA, N_HEADS * QK_DIM), Q_LORA),
        "g_ckv": gain(ks[6], KV_LORA),
        "w_ukv": nrm(ks[7], (DEPTH, KV_LORA, N_HEADS * (NOPE_DIM + V_DIM)), KV_LORA),
        "g_q": gain(ks[8], QK_DIM),
        "g_k": gain(ks[9], QK_DIM),
        "conv_w": nrm(ks[10], (DEPTH, CONV_K, CONV_WIDTH), CONV_K),
        "g_oa": gain(ks[11], ATTN_WIDTH),
        "g_oc": gain(ks[12], CONV_WIDTH),
        "w_o": nrm(ks[13], (DEPTH, D_MODEL, D_MODEL), D_MODEL),
        "w_pl": nrm(ks[14], (DEPTH, PLE_DIM, D_MODEL), PLE_DIM),
        "w_plg": nrm(ks[15], (DEPTH, D_MODEL, D_MODEL), D_MODEL),
        "g_pl": gain(ks[16], D_MODEL),
    }


def reference(x, p, positions, g_in, w_in, g_cq, w_uq, g_ckv, w_ukv, g_q, g_k,
              conv_w, g_oa, g_oc, w_o, w_pl, w_plg, g_pl):
    cos, sin = rope_cos_sin(positions)
    h = x
    for i in range(DEPTH):
        h = hybrid_layer(h, p[i], cos, sin, g_in[i], w_in[i], g_cq[i], w_uq[i],
                         g_ckv[i], w_ukv[i], g_q[i], g_k[i], conv_w[i], g_oa[i],
                         g_oc[i], w_o[i], w_pl[i], w_plg[i], g_pl[i])
    return h.astype(x.dtype)
```

```python
import math
import numpy as np
import concourse.bass as bass
import concourse.mybir as mybir
from concourse.bass_utils import run_bass_kernel_spmd

F32 = mybir.dt.float32
BF16 = mybir.dt.bfloat16
I32 = mybir.dt.int32
AF = mybir.ActivationFunctionType
ALU = mybir.AluOpType
AX = mybir.AxisListType

NEG = -30000.0
WARM = 8
EPS = 1e-6


class Buf:
    __slots__ = ("name", "w", "r", "excl")

    def __init__(self, name, excl=False):
        self.name = name
        self.w = None
        self.r = {}
        self.excl = excl


class T:
    __slots__ = ("h", "b")

    def __init__(self, h, b):
        self.h = h
        self.b = b

    def __getitem__(self, k):
        return self.h[k]


def _b(x):
    return x.b if isinstance(x, T) else x


class Sched:
    ENG = ("pe", "act", "dve", "pool", "sp")

    def __init__(self, nc):
        self.nc = nc
        self.sem = {e: nc.alloc_semaphore("s_" + e) for e in self.ENG}
        self.cnt = {e: 0 for e in self.ENG}
        self.seen = {e: {} for e in self.ENG}
        self.prog = {e: [] for e in self.ENG}
        self.pend = {e: False for e in self.ENG}
        self.slots = []

    def slot(self, name):
        s = dict(sem=self.nc.alloc_semaphore("d_" + name), cnt=0)
        self.slots.append(s)
        return s

    def _deps(self, e, reads, writes):
        deps = {}

        def add(s, v):
            if deps.get(s, 0) < v:
                deps[s] = v
        own = self.sem[e]
        for b in reads:
            b = _b(b)
            if b.w is not None:
                add(*b.w)
            if b.excl:
                for s, v in b.r.items():
                    if s is not own:
                        add(s, v)
        skip_own = (e == "pe")
        for b in writes:
            b = _b(b)
            if b.w is not None and not (skip_own and b.w[0] is own):
                add(*b.w)
            for s, v in b.r.items():
                if not (skip_own and s is own):
                    add(s, v)
        out = []
        for s, v in deps.items():
            if self.seen[e].get(s, 0) < v:
                self.seen[e][s] = v
                out.append((s, v))
        return out

    def _mark(self, tok, reads, writes):
        for b in reads:
            b = _b(b)
            if b.r.get(tok[0], 0) < tok[1]:
                b.r[tok[0]] = tok[1]
        for b in writes:
            b = _b(b)
            b.w = tok
            b.r = {}

    def op(self, e, fn, reads=(), writes=(), sig=True):
        waits = self._deps(e, reads, writes)
        if sig:
            self.cnt[e] += 1
            self.pend[e] = False
            tok = (self.sem[e], self.cnt[e])
        else:
            self.pend[e] = True
            tok = (self.sem[e], self.cnt[e] + 1)
        self.prog[e].append((waits, fn, (self.sem[e], 1) if sig else None))
        self._mark(tok, reads, writes)
        return tok

    def dma(self, fn, slot, reads=(), writes=(), e="sp"):
        waits = self._deps(e, reads, writes)
        slot["cnt"] += 16
        tok = (slot["sem"], slot["cnt"])
        self.prog[e].append((waits, fn, (slot["sem"], 16)))
        self._mark(tok, reads, writes)
        return tok

    def dma_group(self, items, slot, e="sp"):
        n = len(items)
        tok = (slot["sem"], slot["cnt"] + 16 * n)
        slot["cnt"] += 16 * n
        for fn, writes in items:
            waits = self._deps(e, (), writes)
            self.prog[e].append((waits, fn, (slot["sem"], 16)))
        for fn, writes in items:
            self._mark(tok, (), writes)
        return tok

    def wait_all(self, e, toks):
        waits = []
        for s, v in toks:
            if self.seen[e].get(s, 0) < v:
                self.seen[e][s] = v
                waits.append((s, v))
        if waits:
            self.prog[e].append((waits, None, None))

    def barrier(self):
        toks = [(self.sem[e], self.cnt[e]) for e in self.ENG if self.cnt[e] > 0]
        toks += [(s["sem"], s["cnt"]) for s in self.slots if s["cnt"] > 0]
        for e in self.ENG:
            assert not self.pend[e], e
            self.wait_all(e, toks)

    def emit(self):
        nc = self.nc
        for e in self.ENG:
            assert not self.pend[e], e
        with nc.Block() as block:
            def run(eng, lst):
                for waits, fn, inc in lst:
                    for s, v in waits:
                        eng.wait_ge(s, v)
                    if fn is not None:
                        ins = fn(eng)
                        if inc is not None:
                            ins.then_inc(inc[0], inc[1])

            @block.tensor
            def _(eng):
                run(eng, self.prog["pe"])

            @block.scalar
            def _(eng):
                run(eng, self.prog["act"])

            @block.vector
            def _(eng):
                run(eng, self.prog["dve"])

            @block.gpsimd
            def _(eng):
                run(eng, self.prog["pool"])

            @block.sync
            def _(eng):
                run(eng, self.prog["sp"])


_DTSZ = {F32: 4, BF16: 2, I32: 4}


class Arena:
    def __init__(self, nc, lo, hi):
        self.nc, self.lo, self.hi, self.ptr = nc, lo, hi, lo
        self.live, self.dead, self.n = [], [], 0
        self.peak = lo

    def alloc(self, name, shape, dt):
        nb = _DTSZ[dt]
        for d in shape[1:]:
            nb *= d
        nb = (nb + 63) // 64 * 64
        off = self.ptr
        self.ptr += nb
        self.peak = max(self.peak, self.ptr)
        assert self.ptr <= self.hi, (name, self.ptr, self.hi)
        self.n += 1
        h = self.nc.alloc_sbuf_tensor_at("%s_%d" % (name, self.n), list(shape), dt, offset=off)
        b = Buf(name)
        for lo2, hi2, b2 in self.dead:
            if lo2 < off + nb and off < hi2:
                if b2.w is not None and b.r.get(b2.w[0], 0) < b2.w[1]:
                    b.r[b2.w[0]] = b2.w[1]
                for s, v in b2.r.items():
                    if b.r.get(s, 0) < v:
                        b.r[s] = v
        self.live.append((off, off + nb, b))
        return T(h, b)

    def mark(self):
        return (self.ptr, len(self.live))

    def release(self, m):
        ptr, n = m
        self.dead.extend(self.live[n:])
        del self.live[n:]
        self.ptr = ptr


def run_pipelined(gens, skew, maxlive=2):
    pending = list(gens)
    active = []
    since = skew
    while pending or active:
        if pending and len(active) < maxlive and since >= skew:
            active.append(pending.pop(0))
            since = 0
        for g in list(active):
            try:
                next(g)
            except StopIteration:
                active.remove(g)
        since += 1


def bc(ap, axis, n):
    a = ap.unsqueeze(axis)
    shp = list(a.shape)
    shp[axis] = n
    return a.broadcast_to(shp)


def build_program(debug=False, limit=None, ng_kv=16, n_m=4):
    nc = bass.Bass("TRN2", target_bir_lowering=False)
    S = Sched(nc)
    D = {}

    def din(name, shape, dt=F32):
        D[name] = nc.dram_tensor(name, list(shape), dt, kind="ExternalInput").ap()

    din("x_kv", [8192, 1024])
    din("x_own", [16, 128, 1024])
    din("x_halo", [32, 1024])
    din("p_own", [16, 128, 256])
    din("pos_kv", [128, 64], I32)
    din("pos_own", [128, 16], I32)
    din("masks", [128, 8, 128])
    din("ident", [128, 128])
    din("inv_freq", [32])
    din("g_in_pp", [128, 8])
    din("g_pl_pp", [128, 8])
    din("g_o_pp", [128, 8])
    din("g_cq_pp", [128, 2])
    din("conv_wT", [128, 4, 3])
    din("w_in", [1024, 3008])
    din("w_uq", [256, 768])
    din("g_ckv", [128])
    din("w_ukv", [128, 1024])
    din("g_q", [192])
    din("g_k", [192])
    din("w_o", [1024, 1024])
    din("w_pl", [256, 1024])
    din("w_plg", [1024, 1024])
    out = nc.dram_tensor("out_own", [16, 128, 1024], F32, kind="ExternalOutput").ap()

    base = (nc.sbuf_base + 63) // 64 * 64
    AR = Arena(nc, base, nc.sbuf_top)
    A = AR.alloc

    bk = [T(nc.alloc_psum_tensor("bk%d" % i, [128, 512], F32), Buf("bk%d" % i, excl=True)) for i in range(8)]

    def bkb(i):
        return bk[i].h[:].bitcast(BF16)

    w_own = A("w_own", [128, 8, 2816], BF16)
    w_kv = A("w_kv", [128, 8, 192], BF16)
    w_uq = A("w_uq", [128, 2, 768], BF16)
    w_uk = A("w_uk", [128, 4, 128], BF16)
    Aabs = A("Aabs", [128, 4, 128], BF16)
    w_uv = A("w_uv", [128, 4, 128], BF16)
    KcT = A("KcT", [128, 8192], BF16)
    KrT = A("KrT", [128, 4096], BF16)
    Vp = A("Vp", [128, 64, 128], BF16)
    rstdk = A("rstdk", [128, 64, 4], F32)
    idf = A("idf", [128, 128], F32)
    idb = A("idb", [128, 128], BF16)
    maskb = A("maskb", [128, 8, 128], BF16)
    invf = A("invf", [128, 32], F32)
    g_in_pp = A("g_in_pp", [128, 8], F32)
    g_pl_pp = A("g_pl_pp", [128, 8], F32)
    g_o_pp = A("g_o_pp", [128, 8], F32)
    g_cq_pp = A("g_cq_pp", [128, 2], F32)
    cw = A("cw", [128, 4, 3], F32)
    gq_pp = A("gq_pp", [128, 1], F32)
    gk_pp = A("gk_pp", [128, 1], F32)
    gqk_pp = A("gqk_pp", [128, 1], F32)
    g_ckv_b = A("g_ckv_b", [128, 128], F32)
    g_kr_b = A("g_kr_b", [128, 64], F32)
    g_qr_b = A("g_qr_b", [128, 64], F32)
    epsb = A("epsb", [128, 1], F32)
    b192 = A("b192", [128, 1], F32)
    zerob = A("zerob", [128, 1], F32)
    ones_bf = A("ones_bf", [128, 128], BF16)
    ones_f = A("ones_f", [128, 1], F32)
    cosQ = A("cosQ", [128, 16, 32], F32)
    sinQ = A("sinQ", [128, 16, 32], F32)
    xThalo = A("xThalo", [128, 8, 16, 2], BF16)
    QpT = A("QpT", [128, 4, 512], BF16)
    QrT = A("QrT", [128, 2, 4, 512], BF16)
    sza = A("sza", [128, 4, 512], BF16)
    ycT = A("ycT", [128, 4, 512], BF16)
    poso_i = A("poso_i", [128, 16], I32)
    m_regionB = AR.mark()
    NXT = 3
    xt = [A("xt%d" % i, [128, 1024], F32) for i in range(NXT)]

    sl_const = S.slot("const")
    sl_const2 = S.slot("const2")
    sl_x = [S.slot("x%d" % i) for i in range(4)]
    for t_ in range(NXT):
        S.dma(lambda e, t_=t_: e.dma_start(out=xt[t_][:], in_=D["x_kv"][t_ * 128:(t_ + 1) * 128, :]), sl_x[t_],
              (), [xt[t_]])
    sl_w = S.slot("wst")
    sl_out = [S.slot("o0"), S.slot("o1")]
    sl_p = [S.slot("p0"), S.slot("p1")]

    def act(out_, in_, func, reads, writes, scale=1.0, bias=None, accum=None, sig=True):
        kw = dict(out=out_, in_=in_, func=func, scale=scale)
        if bias is not None:
            kw["bias"] = bias
        if accum is not None:
            kw["accum_out"] = accum
        return S.op("act", lambda e: e.activation(**kw), reads, writes, sig)

    def tt(eng, out_, in0, in1, op, reads, writes):
        return S.op(eng, lambda e: e.tensor_tensor(out=out_, in0=in0, in1=in1, op=op), reads, writes)

    def ts(eng, out_, in0, s1, s2, op0, op1, reads, writes):
        if s2 is None:
            return S.op(eng, lambda e: e.tensor_scalar(out=out_, in0=in0, scalar1=s1, scalar2=None, op0=op0),
                        reads, writes)
        return S.op(eng, lambda e: e.tensor_scalar(out=out_, in0=in0, scalar1=s1, scalar2=s2, op0=op0, op1=op1),
                    reads, writes)

    def stt(eng, out_, in0, scalar, in1, op0, op1, reads, writes):
        return S.op(eng, lambda e: e.scalar_tensor_tensor(out=out_, in0=in0, scalar=scalar, in1=in1,
                                                          op0=op0, op1=op1), reads, writes)

    def cp(eng, out_, in_, reads, writes):
        if eng == "act":
            return S.op("act", lambda e: e.copy(out=out_, in_=in_), reads, writes)
        return S.op(eng, lambda e: e.tensor_copy(out=out_, in_=in_), reads, writes)

    def sigmoid_act(out_, in_, reads, writes):
        act(out_, in_, AF.Exp, reads + [zerob], writes, scale=-1.0, bias=zerob[:])
        act(out_, out_, AF.Ln, writes + [ones_f], writes, scale=1.0, bias=ones_f[:])
        act(out_, out_, AF.Exp, writes + [zerob], writes, scale=-1.0, bias=zerob[:])

    def recip(out_, in_, reads, writes):
        return S.op("dve", lambda e: e.reciprocal(out=out_, in_=in_), reads, writes)

    def red(out_, in_, reads, writes, eng="dve"):
        return S.op(eng, lambda e: e.tensor_reduce(out=out_, in_=in_, axis=AX.X, op=ALU.add), reads, writes)

    def mm(out_, lhsT, rhs, start, stop, reads, writes, sig=True):
        return S.op("pe", lambda e: e.matmul(out_, lhsT=lhsT, rhs=rhs, start=start, stop=stop), reads, writes, sig)

    def pe_warm(bank, rhs_ap, rhs_t, n):
        for _ in range(n):
            mm(bank.h[:, :], idb[:], rhs_ap, True, True, [idb, rhs_t], [bank], sig=False)

    def tr(out_, in_, ident, reads, writes, sig=True):
        return S.op("pe", lambda e: e.transpose(out=out_, in_=in_, identity=ident), reads, writes, sig)

    def rstd_from(ssq_ap, out_ap, n, reads, writes, bias_t=None):
        P = out_ap.shape[0]
        act(out_ap, ssq_ap, AF.Ln, reads + [epsb], writes, scale=1.0 / n, bias=epsb[0:P])
        bt = zerob if bias_t is None else bias_t
        act(out_ap, out_ap, AF.Exp, writes + [bt], writes, scale=-0.5, bias=bt[0:P])

    cosK = A("cosK", [128, 64, 32], F32)
    sinK = A("sinK", [128, 64, 32], F32)
    wst = A("wst", [128, 1504], F32)
    m_setup = AR.mark()
    masks_f = A("masks_f", [128, 8, 128], F32)
    posk_i = A("posk_i", [128, 64], I32)
    xh_f = A("xh_f", [32, 1024], F32)
    items = [
        (lambda e: e.dma_start(out=idf[:], in_=D["ident"]), [idf]),
        (lambda e: e.dma_start(out=masks_f[:], in_=D["masks"]), [masks_f]),
        (lambda e: e.dma_start(out=invf[:], in_=D["inv_freq"].partition_broadcast(128)), [invf]),
        (lambda e: e.dma_start(out=posk_i[:], in_=D["pos_kv"]), [posk_i]),
        (lambda e: e.dma_start(out=poso_i[:], in_=D["pos_own"]), [poso_i]),
        (lambda e: e.dma_start(out=g_in_pp[:], in_=D["g_in_pp"]), [g_in_pp]),
        (lambda e: e.dma_start(out=g_pl_pp[:], in_=D["g_pl_pp"]), [g_pl_pp]),
        (lambda e: e.dma_start(out=g_o_pp[:], in_=D["g_o_pp"]), [g_o_pp]),
        (lambda e: e.dma_start(out=g_cq_pp[:], in_=D["g_cq_pp"]), [g_cq_pp]),
        (lambda e: e.dma_start(out=cw[:], in_=D["conv_wT"]), [cw]),
        (lambda e: e.dma_start(out=gq_pp[:], in_=D["g_q"][0:128].rearrange("(p o) -> p o", o=1)), [gq_pp]),
        (lambda e: e.dma_start(out=gk_pp[:], in_=D["g_k"][0:128].rearrange("(p o) -> p o", o=1)), [gk_pp]),
        (lambda e: e.dma_start(out=g_ckv_b[:], in_=D["g_ckv"].partition_broadcast(128)), [g_ckv_b]),
        (lambda e: e.dma_start(out=g_kr_b[:], in_=D["g_k"][128:192].partition_broadcast(128)), [g_kr_b]),
        (lambda e: e.dma_start(out=g_qr_b[:], in_=D["g_q"][128:192].partition_broadcast(128)), [g_qr_b]),
        (lambda e: e.dma_start(out=xh_f[:], in_=D["x_halo"]), [xh_f]),
    ]
    late = [it_ for it_ in items if it_[1][0] is masks_f or it_[1][0] is xh_f]
    early = [it_ for it_ in items if not (it_[1][0] is masks_f or it_[1][0] is xh_f)]
    S.dma_group(early, sl_const)
    S.dma_group(late, sl_const2)
    S.op("pool", lambda e: e.memset(epsb[:], EPS), (), [epsb])
    S.op("pool", lambda e: e.memset(b192[:], -0.5 * math.log(192.0)), (), [b192])
    S.op("pool", lambda e: e.memset(zerob[:], 0.0), (), [zerob])
    S.op("pool", lambda e: e.memset(ones_bf[:], 1.0), (), [ones_bf])
    S.op("pool", lambda e: e.memset(ones_f[:], 1.0), (), [ones_f])
    S.op("pool", lambda e: e.memset(QrT[:], 0.0), (), [QrT])
    cp("pool", idb[:], idf[:], [idf], [idb])
    cp("pool", maskb[:], masks_f[:], [masks_f], [maskb])
    tt("dve", gqk_pp[:], gq_pp[:], gk_pp[:], ALU.mult, [gq_pp, gk_pp], [gqk_pp])

    TWO_PI = 2.0 * math.pi
    SIN_SCALE = 6.283185

    def rope_tables(pos_i, ntile, cos_t, sin_t):
        mk = AR.mark()
        n = ntile * 32
        posf = A("posf", [128, ntile], F32)
        ang = A("ang", [128, ntile, 32], F32)
        cp("dve", posf[:], pos_i[:], [pos_i], [posf])
        tt("dve", ang[:], bc(posf[:], 2, 32), bc(invf[:], 1, ntile), ALU.mult, [posf, invf], [ang])
        angf = ang[:].rearrange("p a b -> p (a b)")
        u = A("u", [128, n], F32)
        nf = A("nf", [128, n], F32)
        for tab, off, eng in ((sin_t, 0.0, "dve"), (cos_t, 0.25, "dve")):
            ni = A("ni", [128, n], I32)
            ts(eng, u[:], angf, 1.0 / TWO_PI, off, ALU.mult, ALU.add, [ang], [u])
            cp(eng, ni[:], u[:], [u], [ni])
            cp(eng, nf[:], ni[:], [ni], [nf])
            tt(eng, u[:], u[:], nf[:], ALU.subtract, [u, nf], [u])
            if eng == "dve":
                stt(eng, nf[:], u[:], 0.5, u[:], ALU.is_gt, ALU.subtract, [u], [nf])
                stt(eng, u[:], nf[:], 0.5, nf[:], ALU.is_gt, ALU.subtract, [nf], [u])
            else:
                ts(eng, nf[:], u[:], 0.5, None, ALU.is_gt, None, [u], [nf])
                tt(eng, u[:], u[:], nf[:], ALU.subtract, [u, nf], [u])
                ts(eng, nf[:], u[:], -0.5, None, ALU.is_lt, None, [u], [nf])
                tt(eng, u[:], u[:], nf[:], ALU.add, [u, nf], [u])
            act(tab[:].rearrange("p a b -> p (a b)"), u[:], AF.Sin, [u, zerob], [tab], scale=SIN_SCALE, bias=zerob[:])
        AR.release(mk)


    def finish():
        S.barrier()
        S.emit()
        return nc, AR.peak
    if limit == "tables":
        return finish()

    rope_tables(posk_i, 64, cosK, sinK)

    mk = AR.mark()
    hj = A("hj", [32, 1024], BF16)
    hs = A("hs", [32, 1], F32)
    hb = A("hb", [32, 1024], BF16)
    act(hj[:], xh_f[:], AF.Square, [xh_f], [hj, hs], accum=hs[:])
    rstd_from(hs[:], hs[:], 1024, [hs], [hs])
    ts("dve", hb[:], xh_f[:], hs[:], None, ALU.mult, None, [xh_f, hs], [hb])
    pth = bkb(0).rearrange("p (c t) -> p c t", c=8)
    for c in range(8):
        tr(pth[:, c, 0:32], hb[:, c * 128:(c + 1) * 128], idb[0:32, 0:32], [hb, idb], [bk[0]], sig=(c == 7))
    cp("dve", xThalo[:].rearrange("p c t h -> p c (t h)"), pth[:, :, 0:32], [bk[0]], [xThalo])
    AR.release(mk)
    AR.release(m_setup)

    for hf in range(2):
        src = D["w_in"][hf * 512:(hf + 1) * 512, 256:448].rearrange("(c p) n -> p c n", p=128)
        dstv = wst[:, 0:768].rearrange("p (c n) -> p c n", c=4)
        S.dma(lambda e, src=src, dstv=dstv: e.dma_start(out=dstv, in_=src), sl_w, (), [wst])
        tt("dve", w_kv[:, hf * 4:(hf + 1) * 4, :], dstv, bc(g_in_pp[:, hf * 4:(hf + 1) * 4], 2, 192), ALU.mult,
           [wst, g_in_pp], [w_kv])
    S.dma(lambda e: e.dma_start(out=wst[:, 0:1024], in_=D["w_ukv"]), sl_w, (), [wst])
    wukv = wst[:, 0:1024].rearrange("p (h n) -> p h n", h=4)
    cp("dve", w_uk[:], wukv[:, :, 0:128], [wst], [w_uk])
    cp("dve", w_uv[:], wukv[:, :, 128:256], [wst], [w_uv])
    for h in range(4):
        S.op("pe", lambda e, h=h: e.transpose(out=bk[1].h[:, h * 128:(h + 1) * 128], in_=wukv[:, h, 0:128],
                                             identity=idf[:]), [wst, idf], [bk[1]], sig=(h == 3))
    ts("dve", Aabs[:].rearrange("p h n -> p (h n)"), bk[1].h[:, :], gqk_pp[:], None, ALU.mult, None,
       [bk[1], gqk_pp], [Aabs])

    if limit == "setup":
        return finish()
    wsts = [wst, A("wst_b", [128, 1504], F32)]
    sl_ws = [sl_w, S.slot("wst_b")]
    job_list = []
    wdst_buf = Buf("wdst")
    gain_src = Buf("gains")
    gain_src.w = g_in_pp.b.w
    for c in range(8):
        rows = D["w_in"][c * 128:(c + 1) * 128, :]
        job_list.append((rows[:, 0:1504], 1504,
                         [(w_own[:, c, 0:256], 0, 256), (w_own[:, c, 256:1312], 448, 1504)], g_in_pp[:, c:c + 1]))
        job_list.append((rows[:, 1504:3008], 1504, [(w_own[:, c, 1312:2816], 0, 1504)], g_in_pp[:, c:c + 1]))
    for c in range(2):
        job_list.append((D["w_uq"][c * 128:(c + 1) * 128, :], 768, [(w_uq[:, c, :], 0, 768)], g_cq_pp[:, c:c + 1]))
    job_state = dict(issued=0, done=0)

    def job_issue():
        k = job_state["issued"]
        if k >= len(job_list):
            return
        src_ap, ncols, pieces, gain_ap = job_list[k]
        w_ = wsts[k % 2]
        S.dma(lambda e: e.dma_start(out=w_[:, 0:ncols], in_=src_ap), sl_ws[k % 2], (), [w_])
        job_state["issued"] = k + 1

    def job_consume():
        k = job_state["done"]
        if k >= len(job_list):
            return
        src_ap, ncols, pieces, gain_ap = job_list[k]
        w_ = wsts[k % 2]
        for dst, lo, hi in pieces:
            ts("dve", dst, w_[:, lo:hi], gain_ap, None, ALU.mult, None, [w_, gain_src], [wdst_buf])
        job_state["done"] = k + 1
        job_issue()

    xb = [A("xb0", [128, 1024], BF16), A("xb1", [128, 1024], BF16)]
    sqj1 = A("sqj", [128, 1024], BF16)
    sqjs = [sqj1, sqj1]
    xTg = [A("xTg0", [128, 8, 512], BF16), A("xTg1", [128, 8, 512], BF16)]
    ssqx = [A("ssqx0", [128, 4], F32), A("ssqx1", [128, 4], F32)]
    rstdx = [A("rstdx0", [128, 4], F32), A("rstdx1", [128, 4], F32)]
    csts = [A("cst0", [128, 4, 192], F32), A("cst1", [128, 4, 192], F32)]
    scr = A("scr", [128, 768], F32)
    sqk = A("sqk", [128, 4, 512], BF16)
    ssqc = A("ssqc", [128, 4], F32)
    ssqps = [A("ssqp0", [128, 4], F32), A("ssqp1", [128, 4], F32)]
    rstdc = A("rstdc", [128, 4], F32)
    vtmp = A("vtmp", [128, 4, 128], F32)
    ssqkn = A("ssqkn", [128, 4, 4], F32)
    krg = A("krg", [128, 4, 64], F32)
    ra = A("ra", [128, 4, 64], F32)
    rb = A("rb", [128, 4, 64], F32)
    krd = A("krd", [128, 4, 128], BF16)
    NG_KV = ng_kv
    NT = 4 * NG_KV

    def ld_x(t):
        S.dma(lambda e: e.dma_start(out=xt[t % NXT][:], in_=D["x_kv"][t * 128:(t + 1) * 128, :]), sl_x[t % NXT],
              (), [xt[t % NXT]])

    def f_sc(t):
        G, tq, tp = t // 4, t % 4, t % 2
        gp = G % 2
        sqj = sqjs[tp]
        xt_ = xt[t % NXT]
        act(sqj[:], xt_[:], AF.Square, [xt_], [sqj, ssqx[gp]], accum=ssqx[gp][:, tq:tq + 1])
        cp("dve", xb[tp][:], xt_[:], [xt_], [xb[tp]])
        if t + NXT < NT:
            ld_x(t + NXT)

    def f_te(t):
        G, tq, tp = t // 4, t % 4, t % 2
        gp = G % 2
        ptr = bkb(tp).rearrange("p (c t) -> p c t", c=8)
        for c in range(8):
            tr(ptr[:, c, :], xb[tp][:, c * 128:(c + 1) * 128], idb[:], [xb[tp], idb], [bk[tp]], sig=(c == 7))
        cp("dve", xTg[gp][:, :, tq * 128:(tq + 1) * 128], ptr, [bk[tp]], [xTg[gp]])

    def kv_front(G):
        for tq in range(4):
            t = 4 * G + tq
            if t + 1 < NT:
                f_sc(t + 1)
            yield
            f_te(t)
            yield

    def kv_back_a(G):
        gp = G % 2
        cst = csts[gp]
        ssqp = ssqps[gp]
        for tq in range(4):
            pcv = bk[2 + tq // 2].h[:, (tq % 2) * 192:(tq % 2) * 192 + 192]
            for c in range(8):
                mm(pcv, xTg[gp][:, c, tq * 128:(tq + 1) * 128], w_kv[:, c, :], c == 0, c == 7,
                   [xTg[gp], w_kv], [bk[2 + tq // 2]], sig=(c == 7))
            if tq % 2:
                yield
        rstd_from(ssqx[gp][:], rstdx[gp][:], 1024, [ssqx[gp]], [rstdx[gp]])
        yield
        for tq in range(4):
            pcv = bk[2 + tq // 2].h[:, (tq % 2) * 192:(tq % 2) * 192 + 192]
            act(cst[:, tq, :], pcv, AF.Copy, [bk[2 + tq // 2], rstdx[gp]], [cst], scale=rstdx[gp][:, tq:tq + 1])
            if tq % 2:
                yield
        cstf = cst[:].rearrange("p a b -> p (a b)")
        tt("dve", scr[:], cstf, cstf, ALU.mult, [cst], [scr])
        yield
        scr3 = scr[:].rearrange("p (a b) -> p a b", a=4)
        red(ssqc[:], scr3[:, :, 0:128], [scr], [ssqc])
        red(ssqp[:], scr3[:, :, 128:192], [scr], [ssqp])
        yield
        rstd_from(ssqc[:], rstdc[:], 128, [ssqc], [rstdc])
        yield
        tt("dve", vtmp[:], cst[:, :, 0:128], bc(rstdc[:], 2, 128), ALU.mult, [cst, rstdc], [vtmp])
        yield
        tt("dve", Vp[:, 4 * G:4 * G + 4, :], vtmp[:], bc(g_ckv_b[:], 1, 4), ALU.mult, [vtmp, g_ckv_b], [Vp])
        yield
        ptv = bkb(6).rearrange("p (c t) -> p c t", c=8)
        for tq in range(4):
            tr(ptv[:, tq, :], Vp[:, 4 * G + tq, :], idb[:], [Vp, idb], [bk[6]], sig=(tq == 3))
        yield
        cp("act", KcT[:, G * 512:(G + 1) * 512].rearrange("p (a b) -> p a b", a=4), ptv[:, 0:4, :], [bk[6]], [KcT])
        job_consume()

    def kv_back_b(G):
        gp = G % 2
        cst = csts[gp]
        ssqp = ssqps[gp]
        tt("pool", krg[:], cst[:, :, 128:192], bc(g_kr_b[:], 1, 4), ALU.mult, [cst, g_kr_b], [krg])
        cK = cosK[:, 4 * G:4 * G + 4, :]
        sK = sinK[:, 4 * G:4 * G + 4, :]
        tt("pool", ra[:, :, 0:32], krg[:, :, 0:32], cK, ALU.mult, [krg, cosK], [ra])
        tt("pool", ra[:, :, 32:64], krg[:, :, 32:64], cK, ALU.mult, [krg, cosK], [ra])
        yield
        for tq in range(4):
            pb = bk[4 + tq % 2]
            mm(pb.h[:, :], KcT[:, (4 * G + tq) * 128:(4 * G + tq + 1) * 128], w_uk[:].rearrange("p h n -> p (h n)"),
               True, True, [KcT, w_uk], [pb])
            if tq == 0:
                tt("pool", rb[:, :, 0:32], krg[:, :, 32:64], sK, ALU.mult, [krg, sinK], [rb])
                tt("pool", rb[:, :, 32:64], krg[:, :, 0:32], sK, ALU.mult, [krg, sinK], [rb])
            if tq == 1:
                tt("pool", krd[:, :, 0:32], ra[:, :, 0:32], rb[:, :, 0:32], ALU.subtract, [ra, rb], [krd])
                tt("pool", krd[:, :, 32:64], ra[:, :, 32:64], rb[:, :, 32:64], ALU.add, [ra, rb], [krd])
            if tq == 2:
                cp("pool", krd[:, :, 64:128], krd[:, :, 0:64], [krd], [krd])
            yield
            act(sqk[:, tq, :], pb.h[:, :], AF.Square, [pb], [sqk])
            yield
        red(ssqkn[:].rearrange("p a b -> p (a b)"), sqk[:].rearrange("p a (h n) -> p (a h) n", h=4), [sqk], [ssqkn])
        yield
        tt("dve", ssqkn[:], ssqkn[:], bc(ssqp[:], 2, 4), ALU.add, [ssqkn, ssqp], [ssqkn])
        yield
        rk = rstdk[:, 4 * G:4 * G + 4, :]
        act(rk, ssqkn[:], AF.Ln, [ssqkn, epsb], [rstdk], scale=1.0 / 192, bias=epsb[:])
        act(rk, rk, AF.Exp, [rstdk, b192], [rstdk], scale=-0.5, bias=b192[:])
        yield
        ptk = bkb(7).rearrange("p (c t) -> p c t", c=8)
        for tq in range(4):
            tr(ptk[:, tq, :], krd[:, tq, :], idb[:], [krd, idb], [bk[7]], sig=(tq == 3))
        yield
        hp = 0 if G < 8 else 64
        col0 = (G % 8) * 512
        cp("act", KrT[hp:hp + 64, col0:col0 + 512].rearrange("p (a b) -> p a b", a=4), ptk[hp:hp + 64, 0:4, :],
           [bk[7]], [KrT])
        job_consume()

    job_issue()
    job_issue()
    f_sc(0)
    for r in range(NG_KV + 2):
        gens = []
        if r < NG_KV:
            gens.append(kv_front(r))
        if 0 <= r - 1 < NG_KV:
            gens.append(kv_back_a(r - 1))
        if 0 <= r - 2 < NG_KV:
            gens.append(kv_back_b(r - 2))
        run_pipelined(gens, skew=0, maxlive=3)
    while job_state["done"] < len(job_list):
        job_consume()

    S.barrier()
    if limit == "kv":
        S.emit()
        return nc, AR.peak
    AR.release(m_regionB)

    w_o = A("w_o", [128, 8, 1024], BF16)
    w_plg = A("w_plg", [128, 8, 1024], BF16)
    w_pl = A("w_pl", [128, 2, 1024], BF16)
    wepi_jobs = []
    for (src, dstT, nchunk, gain) in ((D["w_o"], w_o, 8, g_o_pp), (D["w_plg"], w_plg, 8, g_pl_pp),
                                      (D["w_pl"], w_pl, 2, None)):
        for c in range(nchunk):
            wepi_jobs.append((src, dstT, c, gain))
    wepi_state = dict(issued=0, done=0, bufs=None, slots=[S.slot("w2_%d" % i) for i in range(4)])

    def wepi_issue():
        k = wepi_state["issued"]
        if k >= len(wepi_jobs):
            return
        src, dstT, c, gain = wepi_jobs[k]
        ws = wepi_state["bufs"][k % 4]
        S.dma(lambda e: e.dma_start(out=ws[:], in_=src[c * 128:(c + 1) * 128, :]), wepi_state["slots"][k % 4], (), [ws])
        wepi_state["issued"] = k + 1

    def wepi_consume():
        k = wepi_state["done"]
        if k >= len(wepi_jobs):
            return
        src, dstT, c, gain = wepi_jobs[k]
        ws = wepi_state["bufs"][k % 4]
        if gain is None:
            cp("dve", dstT[:, c, :], ws[:], [ws], [dstT])
        elif k % 2:
            act(dstT[:, c, :], ws[:], AF.Copy, [ws, gain], [dstT], scale=gain[:, c:c + 1])
        else:
            ts("dve", dstT[:, c, :], ws[:], gain[:, c:c + 1], None, ALU.mult, None, [ws, gain], [dstT])
        wepi_state["done"] = k + 1
        wepi_issue()

    if limit == "wepi":
        return finish()

    sl_xo = [S.slot("xo0"), S.slot("xo1")]
    sl_xr = [S.slot("xr0"), S.slot("xr1")]
    outst = [A("outst0", [128, 1024], F32), A("outst1", [128, 1024], F32)]
    out_toks = []

    for m in range(n_m):
        mA = AR.mark()
        xTh = A("xTh", [128, 8, 4, 130], BF16)

        mL1 = AR.mark()
        xos = [A("xo0", [128, 1024], F32), A("xo1", [128, 1024], F32)]
        xobs = [A("xob0", [128, 1024], BF16), A("xob1", [128, 1024], BF16)]
        sjs = [A("sj0", [128, 1024], BF16), A("sj1", [128, 1024], BF16)]
        sos = [A("so0", [128, 1], F32), A("so1", [128, 1], F32)]

        def l1_tile(s):
            t = 4 * m + s
            xo, xob, sj, so = xos[s % 2], xobs[s % 2], sjs[s % 2], sos[s % 2]
            S.dma(lambda e: e.dma_start(out=xo[:], in_=D["x_own"][t]), sl_xo[s % 2], (), [xo])
            yield
            act(sj[:], xo[:], AF.Square, [xo], [sj, so], accum=so[:])
            yield
            rstd_from(so[:], so[:], 1024, [so], [so])
            yield
            ts("dve", xob[:], xo[:], so[:], None, ALU.mult, None, [xo, so], [xob])
            yield
            ptr = bkb(s % 2).rearrange("p (c t) -> p c t", c=8)
            for c in range(8):
                tr(ptr[:, c, :], xob[:, c * 128:(c + 1) * 128], idb[:], [xob, idb], [bk[s % 2]], sig=(c == 7))
            yield
            cp("dve", xTh[:, :, s, 2:130], ptr, [bk[s % 2]], [xTh])

        run_pipelined([l1_tile(s) for s in range(4)], skew=1, maxlive=2)
        cp("pool", xTh[:, :, :, 0:2], xThalo[:, :, 4 * m:4 * m + 4, :], [xThalo], [xTh])
        AR.release(mL1)
        if m == 0:
            rope_tables(poso_i, 16, cosQ, sinQ)
        if limit == "A1a":
            return finish()

        mQ = AR.mark()
        cqnT = A("cqnT", [128, 2, 512], BF16)
        qnT = A("qnT", [128, 4, 512], BF16)
        mQ1 = AR.mark()
        cq_sb = A("cq_sb", [128, 4, 256], F32)
        scq = A("scq", [128, 1024], F32)
        s4 = A("s4", [128, 4], F32)
        r4 = A("r4", [128, 4], F32)
        cqn = A("cqn", [128, 4, 256], BF16)
        ezs = [A("ez0", [128, 512], F32), A("ez1", [128, 512], F32)]

        def cq_chain():
            for s in range(4):
                pv = bk[2 + s // 2].h[:, (s % 2) * 256:(s % 2) * 256 + 256]
                for c in range(8):
                    mm(pv, xTh[:, c, s, 2:130], w_own[:, c, 0:256], c == 0, c == 7, [xTh, wdst_buf],
                       [bk[2 + s // 2]], sig=(c == 7))
                yield
            for b2 in range(2):
                cp("act", cq_sb[:, 2 * b2:2 * b2 + 2, :].rearrange("p a b -> p (a b)"), bk[2 + b2].h[:, :],
                   [bk[2 + b2]], [cq_sb])
            yield
            cqf = cq_sb[:].rearrange("p a b -> p (a b)")
            act(scq[:], cqf, AF.Square, [cq_sb], [scq])
            red(s4[:], scq[:].rearrange("p (a b) -> p a b", a=4), [scq], [s4])
            yield
            rstd_from(s4[:], r4[:], 256, [s4], [r4])
            yield
            tt("dve", cqn[:], cq_sb[:], bc(r4[:], 2, 256), ALU.mult, [cq_sb, r4], [cqn])
            yield
            ptq = bkb(4).rearrange("p (c s t) -> p c s t", c=2, s=4)
            for s in range(4):
                for c2 in range(2):
                    tr(ptq[:, c2, s, :], cqn[:, s, c2 * 128:(c2 + 1) * 128], idb[:], [cqn, idb], [bk[4]],
                       sig=(s == 3 and c2 == 1))
            yield
            cp("act", cqnT[:].rearrange("p c t -> p (c t)"), bkb(4), [bk[4]], [cqnT])

        def za_tile(s):
            pz = bk[5 + s % 2]
            ez = ezs[s % 2]
            for c in range(8):
                mm(pz.h[:, :], xTh[:, c, s, 2:130], w_own[:, c, 256:768], c == 0, c == 7, [xTh, wdst_buf], [pz],
                   sig=(c == 7))
            yield
            sigmoid_act(ez[:], pz.h[:, :], [pz], [ez])
            yield
            tt("dve", sza[:, s, :], pz.h[:, :], ez[:], ALU.mult, [pz, ez], [sza])

        run_pipelined([cq_chain()] + [za_tile(s) for s in range(4)], skew=2, maxlive=3)
        AR.release(mQ1)
        if limit == "A1b":
            return finish()

        q_sbs = [A("q_sb0", [128, 4, 192], F32), A("q_sb1", [128, 4, 192], F32)]
        sq3 = A("sq3", [128, 768], F32)
        s4s = [A("s4a", [128, 4], F32), A("s4b", [128, 4], F32)]
        r4s = [A("r4a", [128, 4], F32), A("r4b", [128, 4], F32)]
        qnbs = [A("qnb0", [128, 4, 128], BF16), A("qnb1", [128, 4, 128], BF16)]
        qrs = [A("qr0", [128, 4, 64], F32), A("qr1", [128, 4, 64], F32)]
        qa = A("qa", [128, 4, 64], F32)
        qb = A("qb", [128, 4, 64], F32)
        qrbs = [A("qrb0", [128, 4, 128], BF16), A("qrb1", [128, 4, 128], BF16)]

        def q_tile(s):
            t = 4 * m + s
            par = s % 2
            pa, pb = bk[2 * par], bk[2 * par + 1]
            q_sb, s4_, r4_, qnb, qr, qrb = q_sbs[par], s4s[par], r4s[par], qnbs[par], qrs[par], qrbs[par]
            for c2 in range(2):
                mm(pa.h[:, :], cqnT[:, c2, s * 128:(s + 1) * 128], w_uq[:, c2, 0:512], c2 == 0, c2 == 1,
                   [cqnT, wdst_buf], [pa], sig=False)
            for c2 in range(2):
                mm(pb.h[:, 0:256], cqnT[:, c2, s * 128:(s + 1) * 128], w_uq[:, c2, 512:768], c2 == 0, c2 == 1,
                   [cqnT, wdst_buf], [pb], sig=(c2 == 1))
            yield
            qf = q_sb[:].rearrange("p a b -> p (a b)")
            cp("act", qf[:, 0:512], pa.h[:, :], [pa], [q_sb])
            cp("act", qf[:, 512:768], pb.h[:, 0:256], [pb], [q_sb])
            yield
            tt("dve", sq3[:], qf, qf, ALU.mult, [q_sb], [sq3])
            red(s4_[:], sq3[:].rearrange("p (a b) -> p a b", a=4), [sq3], [s4_])
            yield
            rstd_from(s4_[:], r4_[:], 192, [s4_], [r4_])
            yield
            tt("dve", qnb[:], q_sb[:, :, 0:128], bc(r4_[:], 2, 128), ALU.mult, [q_sb, r4_], [qnb])
            tt("dve", qr[:], q_sb[:, :, 128:192], bc(r4_[:], 2, 64), ALU.mult, [q_sb, r4_], [qr])
            yield
            tt("dve", qr[:], qr[:], bc(g_qr_b[:], 1, 4), ALU.mult, [qr, g_qr_b], [qr])
            cQ = bc(cosQ[:, t, :], 1, 4)
            sQ = bc(sinQ[:, t, :], 1, 4)
            tt("dve", qa[:, :, 0:32], qr[:, :, 0:32], cQ, ALU.mult, [qr, cosQ], [qa])
            tt("dve", qa[:, :, 32:64], qr[:, :, 32:64], cQ, ALU.mult, [qr, cosQ], [qa])
            yield
            tt("dve", qb[:, :, 0:32], qr[:, :, 32:64], sQ, ALU.mult, [qr, sinQ], [qb])
            tt("dve", qb[:, :, 32:64], qr[:, :, 0:32], sQ, ALU.mult, [qr, sinQ], [qb])
            yield
            tt("dve", qrb[:, :, 0:32], qa[:, :, 0:32], qb[:, :, 0:32], ALU.subtract, [qa, qb], [qrb])
            tt("dve", qrb[:, :, 32:64], qa[:, :, 32:64], qb[:, :, 32:64], ALU.add, [qa, qb], [qrb])
            cp("dve", qrb[:, :, 64:128], qrb[:, :, 0:64], [qrb], [qrb])
            yield
            ptn = bkb(4 + par).rearrange("p (c t) -> p c t", c=8)
            for h in range(4):
                tr(ptn[:, h, :], qnb[:, h, :], idb[:], [qnb, idb], [bk[4 + par]], sig=False)
            for h in range(4):
                tr(ptn[:, 4 + h, :], qrb[:, h, :], idb[:], [qrb, idb], [bk[4 + par]], sig=(h == 3))
            yield
            cp("dve", qnT[:, :, s * 128:(s + 1) * 128], ptn[:, 0:4, :], [bk[4 + par]], [qnT])
            cp("act", QrT[0:64, 0, :, s * 128:(s + 1) * 128], ptn[0:64, 4:8, :], [bk[4 + par]], [QrT])
            cp("act", QrT[64:128, 1, :, s * 128:(s + 1) * 128], ptn[64:128, 4:8, :], [bk[4 + par]], [QrT])

        run_pipelined([q_tile(s) for s in range(4)], skew=4, maxlive=2)
        for h in range(4):
            pb_ = bk[6 + h % 2]
            mm(pb_.h[:, :], Aabs[:, h, :], qnT[:, h, :], True, True, [Aabs, qnT], [pb_])
            cp("act" if h % 2 else "dve", QpT[:, h, :], pb_.h[:, :], [pb_], [QpT])
        AR.release(mQ)
        if limit == "A1":
            return finish()

        ccss = [A("ccs0", [128, 2, 130], F32), A("ccs1", [128, 2, 130], F32)]
        prods = [A("prod0", [128, 4, 130], F32), A("prod1", [128, 4, 130], F32)]
        uus = [A("uu0", [128, 4, 128], F32), A("uu1", [128, 4, 128], F32)]
        e2s = [A("e20", [128, 512], F32), A("e21", [128, 512], F32)]
        g1s = [A("g10", [128, 512], F32), A("g11", [128, 512], F32)]
        gT = A("gT", [128, 4, 512], F32)
        sqb4 = A("sqb4", [128, 4, 512], BF16)
        rbc = A("rbc", [128, 512], F32)
        CB, CC, CX, ZC = 768, 1280, 1792, 2304

        def conv_chunk(j):
            par = j % 2
            ccs, prod, uu, e2, g1 = ccss[par], prods[par], uus[par], e2s[par], g1s[par]
            pcc, pcx, pcb, pzc = bk[4 * par], bk[4 * par + 1], bk[4 * par + 2], bk[4 * par + 3]
            for tp in range(2):
                for c in range(8):
                    mm(pcc.h[:, 0:260].rearrange("p (a b) -> p a b", a=2), w_own[:, c, CC + j * 128:CC + (j + 1) * 128],
                       xTh[:, c, 2 * tp:2 * tp + 2, :], c == 0, c == 7, [xTh, wdst_buf], [pcc], sig=(c == 7))
                for c in range(8):
                    mm(pcx.h[:, 0:260].rearrange("p (a b) -> p a b", a=2), w_own[:, c, CX + j * 128:CX + (j + 1) * 128],
                       xTh[:, c, 2 * tp:2 * tp + 2, :], c == 0, c == 7, [xTh, wdst_buf], [pcx], sig=(c == 7))
                yield
                cp("act", ccs[:].rearrange("p a b -> p (a b)"), pcc.h[:, 0:260], [pcc], [ccs])
                yield
                tt("dve", prod[:, 2 * tp:2 * tp + 2, :].rearrange("p a b -> p (a b)"),
                   ccs[:].rearrange("p a b -> p (a b)"), pcx.h[:, 0:260], ALU.mult, [ccs, pcx], [prod])
                yield
            for c in range(8):
                mm(pcb.h[:, :].rearrange("p (a b) -> p a b", a=4), w_own[:, c, CB + j * 128:CB + (j + 1) * 128],
                   xTh[:, c, :, 2:130], c == 0, c == 7, [xTh, wdst_buf], [pcb], sig=(c == 7))
            for c in range(8):
                mm(pzc.h[:, :].rearrange("p (a b) -> p a b", a=4), w_own[:, c, ZC + j * 128:ZC + (j + 1) * 128],
                   xTh[:, c, :, 2:130], c == 0, c == 7, [xTh, wdst_buf], [pzc], sig=(c == 7))
            act(uu[:], prod[:, :, 0:128], AF.Copy, [prod, cw], [uu], scale=cw[:, j, 0:1])
            yield
            stt("dve", uu[:], prod[:, :, 1:129], cw[:, j, 1:2], uu[:], ALU.mult, ALU.add, [prod, cw, uu], [uu])
            sigmoid_act(e2[:], pzc.h[:, :], [pzc], [e2])
            yield
            stt("dve", uu[:], prod[:, :, 2:130], cw[:, j, 2:3], uu[:], ALU.mult, ALU.add, [prod, cw, uu], [uu])
            yield
            tt("dve", e2[:], pzc.h[:, :], e2[:], ALU.mult, [pzc, e2], [e2])
            tt("dve", g1[:], pcb.h[:, :], uu[:].rearrange("p a b -> p (a b)"), ALU.mult, [pcb, uu], [g1])
            yield
            tt("dve", gT[:, j, :], g1[:], e2[:], ALU.mult, [g1, e2], [gT])
            yield
            act(sqb4[:, j, :], gT[:, j, :], AF.Square, [gT], [sqb4])

        run_pipelined([conv_chunk(j) for j in range(4)], skew=5, maxlive=2)
        for j in range(4):
            mm(bk[0].h[:, :], ones_bf[:], sqb4[:, j, :], j == 0, j == 3, [ones_bf, sqb4], [bk[0]], sig=(j == 3))
        act(rbc[:], bk[0].h[:, :], AF.Ln, [bk[0], epsb], [rbc], scale=1.0 / 512, bias=epsb[:])
        act(rbc[:], rbc[:], AF.Exp, [rbc, zerob], [rbc], scale=-0.5, bias=zerob[:])
        for j in range(4):
            tt("dve", ycT[:, j, :], gT[:, j, :], rbc[:], ALU.mult, [gT, rbc], [ycT])
        AR.release(mA)
        if limit == "A2":
            return finish()

        mAt = AR.mark()
        NPT = 6
        OTb = A("OTb", [128, 4, 512], BF16)
        rsum = A("rsum", [128, 4, 4], F32)
        mPT = AR.mark()
        pt = [A("pt%d" % i, [128, 512], BF16) for i in range(NPT)]
        racc = [A("racc0", [128, 512], F32), A("racc1", [128, 512], F32)]
        if m == 0:
            wepi_state["bufs"] = [A("wst2_%d" % i, [128, 1024], F32) for i in range(4)]
            for _ in range(4):
                wepi_issue()
        nk = 16 * m + 16
        it = 0
        def finish_head(hh):
            cp("dve", OTb[:, hh, :], bk[4 + hh % 2].h[:, :], [bk[4 + hh % 2]], [OTb])
            for s_ in range(4):
                mm(bk[6].h[:, s_ * 4 + hh:s_ * 4 + hh + 1], racc[hh % 2][:, s_ * 128:(s_ + 1) * 128], ones_f[:],
                   True, True, [racc[hh % 2], ones_f], [bk[6]], sig=(s_ == 3))

        for h in range(4):
            hp = (h % 2) * 64
            po = bk[4 + h % 2]
            rc = racc[h % 2]

            def s_mm(kb, it_):
                r = kb - 16 * m
                s0 = 0 if r < 0 else r // 4
                c0 = s0 * 128
                ps = bk[it_ % 4]
                kp = 0 if kb < 32 else 64
                kc = (kb % 32) * 128
                mm(ps.h[:, c0:512], KcT[:, kb * 128:(kb + 1) * 128], QpT[:, h, c0:512], True, False, [QpT], [ps],
                   sig=False)
                mm(ps.h[:, c0:512], KrT[:, kc:kc + 128], QrT[:, kp // 64, h, c0:512], False, r < 0,
                   [QrT], [ps], sig=(r < 0))
                if r >= 0:
                    midx = (s0 % 2) * 4 + (r % 4)
                    mm(ps.h[:, c0:c0 + 128], idb[:], maskb[:, midx, :], False, True, [idb, maskb], [ps])
                return c0

            def c0_of(kb):
                r = kb - 16 * m
                return 0 if r < 0 else (r // 4) * 128

            s_mm(0, it)
            if nk > 1:
                s_mm(1, it + 1)
            for kb in range(nk):
                c0 = c0_of(kb)
                ps = bk[it % 4]
                if kb + 2 < nk:
                    s_mm(kb + 2, it + 2)
                ptb = pt[it % NPT]
                act(ptb[:, c0:512], ps.h[:, c0:512], AF.Exp, [ps, zerob], [ptb], scale=rstdk[:, kb, h:h + 1],
                    bias=zerob[:])
                if kb == 0:
                    cp("dve", rc[:], ptb[:], [ptb], [rc])
                else:
                    tt("dve", rc[:, c0:512], rc[:, c0:512], ptb[:, c0:512], ALU.add, [rc, ptb], [rc])
                mm(po.h[:, c0:512], Vp[:, kb, :], ptb[:, c0:512], kb == 0, kb == nk - 1, [ptb], [po],
                   sig=(kb == nk - 1))
                it += 1
                if h > 0 and kb == min(5, nk - 1):
                    finish_head(h - 1)
                if m == 0 and kb % 3 == 2:
                    wepi_consume()
        finish_head(3)
        if m == 0:
            while wepi_state["done"] < len(wepi_jobs):
                wepi_consume()
        cp("dve", rsum[:].rearrange("p a b -> p (a b)"), bk[6].h[:, 0:16], [bk[6]], [rsum])
        recip(rsum[:], rsum[:], [rsum], [rsum])

        if limit == "att":
            return finish()
        AR.release(mPT)
        o1 = A("o1", [128, 512], F32)
        sg1 = A("sg1", [128, 1], F32)
        sg2 = A("sg2", [128, 1], F32)
        sjk = A("sjk", [128, 1024], BF16)
        yb = A("yb", [128, 512], BF16)
        yaT = A("yaT", [128, 4, 128], BF16)
        xrs = [A("xr0", [128, 1024], F32), A("xr1", [128, 1024], F32)]
        x1s = [A("x10", [128, 1024], F32), A("x11", [128, 1024], F32)]
        x1b = A("x1b", [128, 1024], BF16)
        x1T = A("x1T", [128, 8, 128], BF16)
        gt = A("gt", [128, 1024], F32)
        pfs = [A("pf0", [128, 256], F32), A("pf1", [128, 256], F32)]
        pb16 = A("pb16", [128, 256], BF16)
        pTt = A("pTt", [128, 2, 128], BF16)

        def epi_tile(s):
            t = 4 * m + s
            xr, x1, pf = xrs[s % 2], x1s[s % 2], pfs[s % 2]
            S.dma(lambda e: e.dma_start(out=xr[:], in_=D["x_own"][t]), sl_xr[s % 2], (), [xr])
            S.dma(lambda e: e.dma_start(out=pf[:], in_=D["p_own"][t]), sl_p[s % 2], (), [pf])
            yield
            pov = bk[6]
            for h in range(4):
                mm(pov.h[:, h * 128:(h + 1) * 128], OTb[:, h, s * 128:(s + 1) * 128], w_uv[:, h, :], True, True,
                   [OTb, w_uv], [pov], sig=(h == 3))
            yield
            tt("dve", o1[:].rearrange("p (a b) -> p a b", a=4), pov.h[:, :].rearrange("p (a b) -> p a b", a=4),
               bc(rsum[:, s, :], 2, 128), ALU.mult, [pov, rsum], [o1])
            tt("dve", o1[:], o1[:], sza[:, s, :], ALU.mult, [o1, sza], [o1])
            yield
            act(sjk[:, 0:512], o1[:], AF.Square, [o1], [sjk, sg1], accum=sg1[:])
            rstd_from(sg1[:], sg1[:], 512, [sg1], [sg1])
            yield
            ts("dve", yb[:], o1[:], sg1[:], None, ALU.mult, None, [o1, sg1], [yb])
            yield
            pty = bkb(7).rearrange("p (c t) -> p c t", c=8)
            for c in range(4):
                tr(pty[:, c, :], yb[:, c * 128:(c + 1) * 128], idb[:], [yb, idb], [bk[7]], sig=(c == 3))
            cp("dve", yaT[:], pty[:, 0:4, :], [bk[7]], [yaT])
            yield
            if WARM:
                pe_warm(bk[0], w_o[:, 0, 0:512], w_o, WARM)
            for n in range(2):
                pz = bk[n]
                for c in range(8):
                    lhs = yaT[:, c, :] if c < 4 else ycT[:, c - 4, s * 128:(s + 1) * 128]
                    mm(pz.h[:, :], lhs, w_o[:, c, n * 512:(n + 1) * 512], c == 0, c == 7, [yaT, ycT, w_o], [pz],
                       sig=(c == 7))
                tt("dve", x1[:, n * 512:(n + 1) * 512], pz.h[:, :], xr[:, n * 512:(n + 1) * 512], ALU.add,
                   [pz, xr], [x1])
                yield
            act(sjk[:], x1[:], AF.Square, [x1], [sjk, sg2], accum=sg2[:])
            rstd_from(sg2[:], sg2[:], 1024, [sg2], [sg2])
            yield
            ts("dve", x1b[:], x1[:], sg2[:], None, ALU.mult, None, [x1, sg2], [x1b])
            cp("dve", pb16[:], pf[:], [pf], [pb16])
            yield
            ptx = bkb(2).rearrange("p (c t) -> p c t", c=8)
            for c in range(8):
                tr(ptx[:, c, :], x1b[:, c * 128:(c + 1) * 128], idb[:], [x1b, idb], [bk[2]], sig=(c == 7))
            cp("dve", x1T[:], ptx, [bk[2]], [x1T])
            yield
            ptp = bkb(5).rearrange("p (c t) -> p c t", c=8)
            for c in range(2):
                tr(ptp[:, 4 + c, :], pb16[:, c * 128:(c + 1) * 128], idb[:], [pb16, idb], [bk[5]], sig=(c == 1))
            cp("dve", pTt[:], ptp[:, 4:6, :], [bk[5]], [pTt])
            yield
            ost = outst[t % 2]
            if WARM:
                pe_warm(bk[3], w_o[:, 0, 0:512], w_o, WARM)
            for n in range(2):
                pg = bk[3 + n]
                for c in range(8):
                    mm(pg.h[:, :], x1T[:, c, :], w_plg[:, c, n * 512:(n + 1) * 512], c == 0, c == 7, [x1T, w_plg],
                       [pg], sig=(c == 7))
                gts = gt[:, n * 512:(n + 1) * 512]
                sigmoid_act(gts, pg.h[:, :], [pg], [gt])
                yield
                pp = bk[5]
                for c in range(2):
                    mm(pp.h[:, :], pTt[:, c, :], w_pl[:, c, n * 512:(n + 1) * 512], c == 0, c == 1, [pTt, w_pl], [pp],
                       sig=(c == 1))
                yield
                tt("dve", gts, pp.h[:, :], gts, ALU.mult, [pp, gt], [gt])
                tt("dve", ost[:, n * 512:(n + 1) * 512], gts, x1[:, n * 512:(n + 1) * 512], ALU.add, [gt, x1], [ost])
                yield
            tok = S.dma(lambda e: e.dma_start(out=out[t], in_=ost[:]), sl_out[t % 2], [ost], ())
            out_toks.append(tok)

        run_pipelined([epi_tile(s) for s in range(4)], skew=8, maxlive=2)
        AR.release(mAt)

    S.wait_all("sp", out_toks[-2:])
    S.emit()
    return nc, AR.peak


_CACHE = {}


def _get_program():
    if "nc" not in _CACHE:
        _CACHE["nc"] = build_program()[0]
    return _CACHE["nc"]


def _host_inputs(x, p, positions, g_in, w_in, g_cq, w_uq, g_ckv, w_ukv, g_q, g_k, conv_w, g_oa, g_oc, w_o, w_pl,
                 w_plg, g_pl):
    f = lambda a: np.ascontiguousarray(np.asarray(a, dtype=np.float32))
    x = f(x)
    p = f(p)
    positions = np.ascontiguousarray(np.asarray(positions, dtype=np.int32))
    inv_freq = (1.0 / (10000.0 ** (np.arange(0, 64, 2, dtype=np.float32) / np.float32(64)))).astype(np.float32)
    tri = np.where(np.arange(128)[:, None] <= np.arange(128)[None, :], 0.0, NEG).astype(np.float32)
    full = np.full((128, 128), NEG, np.float32)
    zero = np.zeros((128, 128), np.float32)
    shared = dict(
        ident=np.eye(128, dtype=np.float32),
        inv_freq=inv_freq,
        g_in_pp=f(f(g_in)[0].reshape(8, 128).T),
        g_pl_pp=f(f(g_pl)[0].reshape(8, 128).T),
        g_o_pp=f(np.concatenate([f(g_oa)[0], f(g_oc)[0]]).reshape(8, 128).T),
        g_cq_pp=f(f(g_cq)[0].reshape(2, 128).T),
        conv_wT=f(f(conv_w)[0].reshape(3, 4, 128).transpose(2, 1, 0)),
        w_in=f(w_in)[0], w_uq=f(w_uq)[0], g_ckv=f(g_ckv)[0], w_ukv=f(w_ukv)[0], g_q=f(g_q)[0], g_k=f(g_k)[0],
        w_o=f(w_o)[0], w_pl=f(w_pl)[0], w_plg=f(w_plg)[0],
    )
    in_maps, blocks_all = [], []
    for core in range(8):
        b, j = core // 4, core % 4
        blocks = [16 * m + o for m in range(4) for o in (j, 7 - j, 8 + j, 15 - j)]
        blocks_all.append(blocks)
        xb_ = x[b].reshape(64, 128, 1024)
        halo = np.zeros((16, 2, 1024), np.float32)
        for i, blk in enumerate(blocks):
            if blk > 0:
                halo[i] = x[b, blk * 128 - 2:blk * 128]
        pos_b = positions[b].reshape(64, 128)

        def mk(r, target):
            return zero if r < target else (tri if r == target else full)
        masks = np.stack([mk(r, j) for r in range(4)] + [mk(r, 3 - j) for r in range(4)], axis=1)
        d = dict(shared)
        d.update(
            x_kv=x[b],
            x_own=f(xb_[blocks]),
            x_halo=f(halo.reshape(32, 1024)),
            p_own=f(p[0, b].reshape(64, 128, 256)[blocks]),
            pos_kv=np.ascontiguousarray(pos_b.T),
            pos_own=np.ascontiguousarray(pos_b[blocks].T),
            masks=f(masks),
        )
        in_maps.append(d)
    return in_maps, blocks_all


def kernel(**inputs):
    in_maps, blocks_all = _host_inputs(**inputs)
    nc = _get_program()
    res = run_bass_kernel_spmd(nc, in_maps, core_ids=list(range(8)))
    out = np.empty((2, 64, 128, 1024), np.float32)
    for core in range(8):
        o = np.asarray(res.results[core]["out_own"]).reshape(16, 128, 1024)
        out[core // 4, blocks_all[core]] = o
    return out.reshape(2, 8192, 1024)
```

```python
import math
import numpy as np
import concourse.bass as bass
import concourse.mybir as mybir
from concourse.bass_utils import run_bass_kernel_spmd

F32 = mybir.dt.float32
BF16 = mybir.dt.bfloat16
I32 = mybir.dt.int32
AF = mybir.ActivationFunctionType
ALU = mybir.AluOpType
AX = mybir.AxisListType

NEG = -30000.0
WARM = 6
EPS = 1e-6


class Buf:
    __slots__ = ("name", "w", "r", "excl")

    def __init__(self, name, excl=False):
        self.name = name
        self.w = None
        self.r = {}
        self.excl = excl


class T:
    __slots__ = ("h", "b")

    def __init__(self, h, b):
        self.h = h
        self.b = b

    def __getitem__(self, k):
        return self.h[k]


def _b(x):
    return x.b if isinstance(x, T) else x


class Sched:
    ENG = ("pe", "act", "dve", "pool", "sp")

    def __init__(self, nc):
        self.nc = nc
        self.sem = {e: nc.alloc_semaphore("s_" + e) for e in self.ENG}
        self.cnt = {e: 0 for e in self.ENG}
        self.seen = {e: {} for e in self.ENG}
        self.prog = {e: [] for e in self.ENG}
        self.pend = {e: False for e in self.ENG}
        self.slots = []

    def slot(self, name):
        s = dict(sem=self.nc.alloc_semaphore("d_" + name), cnt=0)
        self.slots.append(s)
        return s

    def _deps(self, e, reads, writes):
        deps = {}

        def add(s, v):
            if deps.get(s, 0) < v:
                deps[s] = v
        own = self.sem[e]
        for b in reads:
            b = _b(b)
            if b.w is not None:
                add(*b.w)
            if b.excl:
                for s, v in b.r.items():
                    if s is not own:
                        add(s, v)
        skip_own = (e == "pe")
        for b in writes:
            b = _b(b)
            if b.w is not None and not (skip_own and b.w[0] is own):
                add(*b.w)
            for s, v in b.r.items():
                if not (skip_own and s is own):
                    add(s, v)
        out = []
        for s, v in deps.items():
            if self.seen[e].get(s, 0) < v:
                self.seen[e][s] = v
                out.append((s, v))
        return out

    def _mark(self, tok, reads, writes):
        for b in reads:
            b = _b(b)
            if b.r.get(tok[0], 0) < tok[1]:
                b.r[tok[0]] = tok[1]
        for b in writes:
            b = _b(b)
            b.w = tok
            b.r = {}

    def op(self, e, fn, reads=(), writes=(), sig=True):
        waits = self._deps(e, reads, writes)
        if sig:
            self.cnt[e] += 1
            self.pend[e] = False
            tok = (self.sem[e], self.cnt[e])
        else:
            self.pend[e] = True
            tok = (self.sem[e], self.cnt[e] + 1)
        self.prog[e].append((waits, fn, (self.sem[e], 1) if sig else None))
        self._mark(tok, reads, writes)
        return tok

    def dma(self, fn, slot, reads=(), writes=(), e="sp"):
        waits = self._deps(e, reads, writes)
        slot["cnt"] += 16
        tok = (slot["sem"], slot["cnt"])
        self.prog[e].append((waits, fn, (slot["sem"], 16)))
        self._mark(tok, reads, writes)
        return tok

    def dma_group(self, items, slot, e="sp"):
        n = len(items)
        tok = (slot["sem"], slot["cnt"] + 16 * n)
        slot["cnt"] += 16 * n
        for fn, writes in items:
            waits = self._deps(e, (), writes)
            self.prog[e].append((waits, fn, (slot["sem"], 16)))
        for fn, writes in items:
            self._mark(tok, (), writes)
        return tok

    def wait_all(self, e, toks):
        waits = []
        for s, v in toks:
            if self.seen[e].get(s, 0) < v:
                self.seen[e][s] = v
                waits.append((s, v))
        if waits:
            self.prog[e].append((waits, None, None))

    def barrier(self):
        toks = [(self.sem[e], self.cnt[e]) for e in self.ENG if self.cnt[e] > 0]
        toks += [(s["sem"], s["cnt"]) for s in self.slots if s["cnt"] > 0]
        for e in self.ENG:
            assert not self.pend[e], e
            self.wait_all(e, toks)

    def emit(self):
        nc = self.nc
        for e in self.ENG:
            assert not self.pend[e], e
        with nc.Block() as block:
            def run(eng, lst):
                for waits, fn, inc in lst:
                    for s, v in waits:
                        eng.wait_ge(s, v)
                    if fn is not None:
                        ins = fn(eng)
                        if inc is not None:
                            ins.then_inc(inc[0], inc[1])

            @block.tensor
            def _(eng):
                run(eng, self.prog["pe"])

            @block.scalar
            def _(eng):
                run(eng, self.prog["act"])

            @block.vector
            def _(eng):
                run(eng, self.prog["dve"])

            @block.gpsimd
            def _(eng):
                run(eng, self.prog["pool"])

            @block.sync
            def _(eng):
                run(eng, self.prog["sp"])


_DTSZ = {F32: 4, BF16: 2, I32: 4}


class Arena:
    def __init__(self, nc, lo, hi):
        self.nc, self.lo, self.hi, self.ptr = nc, lo, hi, lo
        self.live, self.dead, self.n = [], [], 0
        self.peak = lo

    def alloc(self, name, shape, dt):
        nb = _DTSZ[dt]
        for d in shape[1:]:
            nb *= d
        nb = (nb + 63) // 64 * 64
        off = self.ptr
        self.ptr += nb
        self.peak = max(self.peak, self.ptr)
        assert self.ptr <= self.hi, (name, self.ptr, self.hi)
        self.n += 1
        h = self.nc.alloc_sbuf_tensor_at("%s_%d" % (name, self.n), list(shape), dt, offset=off)
        b = Buf(name)
        for lo2, hi2, b2 in self.dead:
            if lo2 < off + nb and off < hi2:
                if b2.w is not None and b.r.get(b2.w[0], 0) < b2.w[1]:
                    b.r[b2.w[0]] = b2.w[1]
                for s, v in b2.r.items():
                    if b.r.get(s, 0) < v:
                        b.r[s] = v
        self.live.append((off, off + nb, b))
        return T(h, b)

    def mark(self):
        return (self.ptr, len(self.live))

    def release(self, m):
        ptr, n = m
        self.dead.extend(self.live[n:])
        del self.live[n:]
        self.ptr = ptr


def run_pipelined(gens, skew, maxlive=2):
    pending = list(gens)
    active = []
    since = skew
    while pending or active:
        if pending and len(active) < maxlive and since >= skew:
            active.append(pending.pop(0))
            since = 0
        for g in list(active):
            try:
                next(g)
            except StopIteration:
                active.remove(g)
        since += 1


def bc(ap, axis, n):
    a = ap.unsqueeze(axis)
    shp = list(a.shape)
    shp[axis] = n
    return a.broadcast_to(shp)


def build_program(debug=False, limit=None, ng_kv=16, n_m=4):
    nc = bass.Bass("TRN2", target_bir_lowering=False)
    S = Sched(nc)
    D = {}

    def din(name, shape, dt=F32):
        D[name] = nc.dram_tensor(name, list(shape), dt, kind="ExternalInput").ap()

    din("x_kv", [8192, 1024])
    din("x_own", [16, 128, 1024])
    din("x_halo", [32, 1024])
    din("p_own", [16, 128, 256])
    din("pos_kv", [128, 64], I32)
    din("pos_own", [128, 16], I32)
    din("masks", [128, 8, 128])
    din("ident", [128, 128])
    din("inv_freq", [32])
    din("g_in_pp", [128, 8])
    din("g_pl_pp", [128, 8])
    din("g_o_pp", [128, 8])
    din("g_cq_pp", [128, 2])
    din("conv_wT", [128, 4, 3])
    din("w_in", [1024, 3008])
    din("w_uq", [256, 768])
    din("g_ckv", [128])
    din("w_ukv", [128, 1024])
    din("g_q", [192])
    din("g_k", [192])
    din("w_o", [1024, 1024])
    din("w_pl", [256, 1024])
    din("w_plg", [1024, 1024])
    out = nc.dram_tensor("out_own", [16, 128, 1024], F32, kind="ExternalOutput").ap()

    base = (nc.sbuf_base + 63) // 64 * 64
    AR = Arena(nc, base, nc.sbuf_top)
    A = AR.alloc

    bk = [T(nc.alloc_psum_tensor("bk%d" % i, [128, 512], F32), Buf("bk%d" % i, excl=True)) for i in range(8)]

    def bkb(i):
        return bk[i].h[:].bitcast(BF16)

    w_own = A("w_own", [128, 8, 2816], BF16)
    w_kv = A("w_kv", [128, 8, 192], BF16)
    w_uq = A("w_uq", [128, 2, 768], BF16)
    w_uk = A("w_uk", [128, 4, 128], BF16)
    Aabs = A("Aabs", [128, 4, 128], BF16)
    w_uv = A("w_uv", [128, 4, 128], BF16)
    KcT = A("KcT", [128, 8192], BF16)
    KrT = A("KrT", [128, 4096], BF16)
    Vp = A("Vp", [128, 64, 128], BF16)
    rstdk = A("rstdk", [128, 64, 4], F32)
    idf = A("idf", [128, 128], F32)
    idb = A("idb", [128, 128], BF16)
    maskb = A("maskb", [128, 8, 128], BF16)
    invf = A("invf", [128, 32], F32)
    g_in_pp = A("g_in_pp", [128, 8], F32)
    g_pl_pp = A("g_pl_pp", [128, 8], F32)
    g_o_pp = A("g_o_pp", [128, 8], F32)
    g_cq_pp = A("g_cq_pp", [128, 2], F32)
    cw = A("cw", [128, 4, 3], F32)
    gq_pp = A("gq_pp", [128, 1], F32)
    gk_pp = A("gk_pp", [128, 1], F32)
    gqk_pp = A("gqk_pp", [128, 1], F32)
    g_ckv_b = A("g_ckv_b", [128, 128], F32)
    g_kr_b = A("g_kr_b", [128, 64], F32)
    g_qr_b = A("g_qr_b", [128, 64], F32)
    epsb = A("epsb", [128, 1], F32)
    b192 = A("b192", [128, 1], F32)
    zerob = A("zerob", [128, 1], F32)
    ones_bf = A("ones_bf", [128, 128], BF16)
    ones_f = A("ones_f", [128, 1], F32)
    cosQ = A("cosQ", [128, 16, 32], F32)
    sinQ = A("sinQ", [128, 16, 32], F32)
    xThalo = A("xThalo", [128, 8, 16, 2], BF16)
    QpT = A("QpT", [128, 4, 512], BF16)
    QrT = A("QrT", [128, 2, 4, 512], BF16)
    sza = A("sza", [128, 4, 512], BF16)
    ycT = A("ycT", [128, 4, 512], BF16)
    poso_i = A("poso_i", [128, 16], I32)
    m_regionB = AR.mark()
    NXT = 3
    xt = [A("xt%d" % i, [128, 1024], F32) for i in range(NXT)]

    sl_const = S.slot("const")
    sl_const2 = S.slot("const2")
    sl_x = [S.slot("x%d" % i) for i in range(4)]
    for t_ in range(NXT):
        S.dma(lambda e, t_=t_: e.dma_start(out=xt[t_][:], in_=D["x_kv"][t_ * 128:(t_ + 1) * 128, :]), sl_x[t_],
              (), [xt[t_]])
    sl_w = S.slot("wst")
    sl_out = [S.slot("o0"), S.slot("o1")]
    sl_p = [S.slot("p0"), S.slot("p1")]

    def act(out_, in_, func, reads, writes, scale=1.0, bias=None, accum=None, sig=True):
        kw = dict(out=out_, in_=in_, func=func, scale=scale)
        if bias is not None:
            kw["bias"] = bias
        if accum is not None:
            kw["accum_out"] = accum
        return S.op("act", lambda e: e.activation(**kw), reads, writes, sig)

    def tt(eng, out_, in0, in1, op, reads, writes):
        return S.op(eng, lambda e: e.tensor_tensor(out=out_, in0=in0, in1=in1, op=op), reads, writes)

    def ts(eng, out_, in0, s1, s2, op0, op1, reads, writes):
        if s2 is None:
            return S.op(eng, lambda e: e.tensor_scalar(out=out_, in0=in0, scalar1=s1, scalar2=None, op0=op0),
                        reads, writes)
        return S.op(eng, lambda e: e.tensor_scalar(out=out_, in0=in0, scalar1=s1, scalar2=s2, op0=op0, op1=op1),
                    reads, writes)

    def stt(eng, out_, in0, scalar, in1, op0, op1, reads, writes):
        return S.op(eng, lambda e: e.scalar_tensor_tensor(out=out_, in0=in0, scalar=scalar, in1=in1,
                                                          op0=op0, op1=op1), reads, writes)

    def cp(eng, out_, in_, reads, writes):
        if eng == "act":
            return S.op("act", lambda e: e.copy(out=out_, in_=in_), reads, writes)
        return S.op(eng, lambda e: e.tensor_copy(out=out_, in_=in_), reads, writes)

    def sigmoid_act(out_, in_, reads, writes):
        act(out_, in_, AF.Exp, reads + [zerob], writes, scale=-1.0, bias=zerob[:])
        act(out_, out_, AF.Ln, writes + [ones_f], writes, scale=1.0, bias=ones_f[:])
        act(out_, out_, AF.Exp, writes + [zerob], writes, scale=-1.0, bias=zerob[:])

    def recip(out_, in_, reads, writes):
        return S.op("dve", lambda e: e.reciprocal(out=out_, in_=in_), reads, writes)

    def red(out_, in_, reads, writes, eng="dve"):
        return S.op(eng, lambda e: e.tensor_reduce(out=out_, in_=in_, axis=AX.X, op=ALU.add), reads, writes)

    def mm(out_, lhsT, rhs, start, stop, reads, writes, sig=True):
        return S.op("pe", lambda e: e.matmul(out_, lhsT=lhsT, rhs=rhs, start=start, stop=stop), reads, writes, sig)

    def pe_warm(bank, rhs_ap, rhs_t, n):
        for _ in range(n):
            mm(bank.h[:, :], idb[:], rhs_ap, True, True, [idb, rhs_t], [bank], sig=False)

    def tr(out_, in_, ident, reads, writes, sig=True):
        return S.op("pe", lambda e: e.transpose(out=out_, in_=in_, identity=ident), reads, writes, sig)

    def rstd_from(ssq_ap, out_ap, n, reads, writes, bias_t=None):
        P = out_ap.shape[0]
        act(out_ap, ssq_ap, AF.Ln, reads + [epsb], writes, scale=1.0 / n, bias=epsb[0:P])
        bt = zerob if bias_t is None else bias_t
        act(out_ap, out_ap, AF.Exp, writes + [bt], writes, scale=-0.5, bias=bt[0:P])

    cosK = A("cosK", [128, 64, 32], F32)
    sinK = A("sinK", [128, 64, 32], F32)
    wst = A("wst", [128, 1504], F32)
    m_setup = AR.mark()
    masks_f = A("masks_f", [128, 8, 128], F32)
    posk_i = A("posk_i", [128, 64], I32)
    xh_f = A("xh_f", [32, 1024], F32)
    items = [
        (lambda e: e.dma_start(out=idf[:], in_=D["ident"]), [idf]),
        (lambda e: e.dma_start(out=masks_f[:], in_=D["masks"]), [masks_f]),
        (lambda e: e.dma_start(out=invf[:], in_=D["inv_freq"].partition_broadcast(128)), [invf]),
        (lambda e: e.dma_start(out=posk_i[:], in_=D["pos_kv"]), [posk_i]),
        (lambda e: e.dma_start(out=poso_i[:], in_=D["pos_own"]), [poso_i]),
        (lambda e: e.dma_start(out=g_in_pp[:], in_=D["g_in_pp"]), [g_in_pp]),
        (lambda e: e.dma_start(out=g_pl_pp[:], in_=D["g_pl_pp"]), [g_pl_pp]),
        (lambda e: e.dma_start(out=g_o_pp[:], in_=D["g_o_pp"]), [g_o_pp]),
        (lambda e: e.dma_start(out=g_cq_pp[:], in_=D["g_cq_pp"]), [g_cq_pp]),
        (lambda e: e.dma_start(out=cw[:], in_=D["conv_wT"]), [cw]),
        (lambda e: e.dma_start(out=gq_pp[:], in_=D["g_q"][0:128].rearrange("(p o) -> p o", o=1)), [gq_pp]),
        (lambda e: e.dma_start(out=gk_pp[:], in_=D["g_k"][0:128].rearrange("(p o) -> p o", o=1)), [gk_pp]),
        (lambda e: e.dma_start(out=g_ckv_b[:], in_=D["g_ckv"].partition_broadcast(128)), [g_ckv_b]),
        (lambda e: e.dma_start(out=g_kr_b[:], in_=D["g_k"][128:192].partition_broadcast(128)), [g_kr_b]),
        (lambda e: e.dma_start(out=g_qr_b[:], in_=D["g_q"][128:192].partition_broadcast(128)), [g_qr_b]),
        (lambda e: e.dma_start(out=xh_f[:], in_=D["x_halo"]), [xh_f]),
    ]
    late = [it_ for it_ in items if it_[1][0] is masks_f or it_[1][0] is xh_f]
    early = [it_ for it_ in items if not (it_[1][0] is masks_f or it_[1][0] is xh_f)]
    S.dma_group(early, sl_const)
    S.dma_group(late, sl_const2)
    S.op("pool", lambda e: e.memset(epsb[:], EPS), (), [epsb])
    S.op("pool", lambda e: e.memset(b192[:], -0.5 * math.log(192.0)), (), [b192])
    S.op("pool", lambda e: e.memset(zerob[:], 0.0), (), [zerob])
    S.op("pool", lambda e: e.memset(ones_bf[:], 1.0), (), [ones_bf])
    S.op("pool", lambda e: e.memset(ones_f[:], 1.0), (), [ones_f])
    S.op("pool", lambda e: e.memset(QrT[:], 0.0), (), [QrT])
    cp("pool", idb[:], idf[:], [idf], [idb])
    cp("pool", maskb[:], masks_f[:], [masks_f], [maskb])
    tt("dve", gqk_pp[:], gq_pp[:], gk_pp[:], ALU.mult, [gq_pp, gk_pp], [gqk_pp])

    TWO_PI = 2.0 * math.pi
    SIN_SCALE = 6.283185

    def rope_tables(pos_i, ntile, cos_t, sin_t):
        mk = AR.mark()
        n = ntile * 32
        posf = A("posf", [128, ntile], F32)
        ang = A("ang", [128, ntile, 32], F32)
        cp("dve", posf[:], pos_i[:], [pos_i], [posf])
        tt("dve", ang[:], bc(posf[:], 2, 32), bc(invf[:], 1, ntile), ALU.mult, [posf, invf], [ang])
        angf = ang[:].rearrange("p a b -> p (a b)")
        u = A("u", [128, n], F32)
        nf = A("nf", [128, n], F32)
        for tab, off, eng in ((sin_t, 0.0, "dve"), (cos_t, 0.25, "dve")):
            ni = A("ni", [128, n], I32)
            ts(eng, u[:], angf, 1.0 / TWO_PI, off, ALU.mult, ALU.add, [ang], [u])
            cp(eng, ni[:], u[:], [u], [ni])
            cp(eng, nf[:], ni[:], [ni], [nf])
            tt(eng, u[:], u[:], nf[:], ALU.subtract, [u, nf], [u])
            if eng == "dve":
                stt(eng, nf[:], u[:], 0.5, u[:], ALU.is_gt, ALU.subtract, [u], [nf])
                stt(eng, u[:], nf[:], 0.5, nf[:], ALU.is_gt, ALU.subtract, [nf], [u])
            else:
                ts(eng, nf[:], u[:], 0.5, None, ALU.is_gt, None, [u], [nf])
                tt(eng, u[:], u[:], nf[:], ALU.subtract, [u, nf], [u])
                ts(eng, nf[:], u[:], -0.5, None, ALU.is_lt, None, [u], [nf])
                tt(eng, u[:], u[:], nf[:], ALU.add, [u, nf], [u])
            act(tab[:].rearrange("p a b -> p (a b)"), u[:], AF.Sin, [u, zerob], [tab], scale=SIN_SCALE, bias=zerob[:])
        AR.release(mk)


    def finish():
        S.barrier()
        S.emit()
        return nc, AR.peak
    if limit == "tables":
        return finish()

    rope_tables(posk_i, 64, cosK, sinK)

    mk = AR.mark()
    hj = A("hj", [32, 1024], BF16)
    hs = A("hs", [32, 1], F32)
    hb = A("hb", [32, 1024], BF16)
    act(hj[:], xh_f[:], AF.Square, [xh_f], [hj, hs], accum=hs[:])
    rstd_from(hs[:], hs[:], 1024, [hs], [hs])
    ts("dve", hb[:], xh_f[:], hs[:], None, ALU.mult, None, [xh_f, hs], [hb])
    pth = bkb(0).rearrange("p (c t) -> p c t", c=8)
    for c in range(8):
        tr(pth[:, c, 0:32], hb[:, c * 128:(c + 1) * 128], idb[0:32, 0:32], [hb, idb], [bk[0]], sig=(c == 7))
    cp("dve", xThalo[:].rearrange("p c t h -> p c (t h)"), pth[:, :, 0:32], [bk[0]], [xThalo])
    AR.release(mk)
    AR.release(m_setup)

    for hf in range(2):
        src = D["w_in"][hf * 512:(hf + 1) * 512, 256:448].rearrange("(c p) n -> p c n", p=128)
        dstv = wst[:, 0:768].rearrange("p (c n) -> p c n", c=4)
        S.dma(lambda e, src=src, dstv=dstv: e.dma_start(out=dstv, in_=src), sl_w, (), [wst])
        tt("dve", w_kv[:, hf * 4:(hf + 1) * 4, :], dstv, bc(g_in_pp[:, hf * 4:(hf + 1) * 4], 2, 192), ALU.mult,
           [wst, g_in_pp], [w_kv])
    S.dma(lambda e: e.dma_start(out=wst[:, 0:1024], in_=D["w_ukv"]), sl_w, (), [wst])
    wukv = wst[:, 0:1024].rearrange("p (h n) -> p h n", h=4)
    cp("dve", w_uk[:], wukv[:, :, 0:128], [wst], [w_uk])
    cp("dve", w_uv[:], wukv[:, :, 128:256], [wst], [w_uv])
    for h in range(4):
        S.op("pe", lambda e, h=h: e.transpose(out=bk[1].h[:, h * 128:(h + 1) * 128], in_=wukv[:, h, 0:128],
                                             identity=idf[:]), [wst, idf], [bk[1]], sig=(h == 3))
    ts("dve", Aabs[:].rearrange("p h n -> p (h n)"), bk[1].h[:, :], gqk_pp[:], None, ALU.mult, None,
       [bk[1], gqk_pp], [Aabs])

    if limit == "setup":
        return finish()
    wsts = [wst, A("wst_b", [128, 1504], F32)]
    sl_ws = [sl_w, S.slot("wst_b")]
    job_list = []
    wdst_buf = Buf("wdst")
    gain_src = Buf("gains")
    gain_src.w = g_in_pp.b.w
    for c in range(8):
        rows = D["w_in"][c * 128:(c + 1) * 128, :]
        job_list.append((rows[:, 0:1504], 1504,
                         [(w_own[:, c, 0:256], 0, 256), (w_own[:, c, 256:1312], 448, 1504)], g_in_pp[:, c:c + 1]))
        job_list.append((rows[:, 1504:3008], 1504, [(w_own[:, c, 1312:2816], 0, 1504)], g_in_pp[:, c:c + 1]))
    for c in range(2):
        job_list.append((D["w_uq"][c * 128:(c + 1) * 128, :], 768, [(w_uq[:, c, :], 0, 768)], g_cq_pp[:, c:c + 1]))
    job_state = dict(issued=0, done=0)

    def job_issue():
        k = job_state["issued"]
        if k >= len(job_list):
            return
        src_ap, ncols, pieces, gain_ap = job_list[k]
        w_ = wsts[k % 2]
        S.dma(lambda e: e.dma_start(out=w_[:, 0:ncols], in_=src_ap), sl_ws[k % 2], (), [w_])
        job_state["issued"] = k + 1

    def job_consume():
        k = job_state["done"]
        if k >= len(job_list):
            return
        src_ap, ncols, pieces, gain_ap = job_list[k]
        w_ = wsts[k % 2]
        for dst, lo, hi in pieces:
            ts("dve", dst, w_[:, lo:hi], gain_ap, None, ALU.mult, None, [w_, gain_src], [wdst_buf])
        job_state["done"] = k + 1
        job_issue()

    xb = [A("xb0", [128, 1024], BF16), A("xb1", [128, 1024], BF16)]
    sqj1 = A("sqj", [128, 1024], BF16)
    sqjs = [sqj1, sqj1]
    xTg = [A("xTg0", [128, 8, 512], BF16), A("xTg1", [128, 8, 512], BF16)]
    ssqx = [A("ssqx0", [128, 4], F32), A("ssqx1", [128, 4], F32)]
    rstdx = [A("rstdx0", [128, 4], F32), A("rstdx1", [128, 4], F32)]
    csts = [A("cst0", [128, 4, 192], F32), A("cst1", [128, 4, 192], F32)]
    scr = A("scr", [128, 768], F32)
    sqk = A("sqk", [128, 4, 512], BF16)
    ssqc = A("ssqc", [128, 4], F32)
    ssqps = [A("ssqp0", [128, 4], F32), A("ssqp1", [128, 4], F32)]
    rstdc = A("rstdc", [128, 4], F32)
    vtmp = A("vtmp", [128, 4, 128], F32)
    ssqkn = A("ssqkn", [128, 4, 4], F32)
    krg = A("krg", [128, 4, 64], F32)
    ra = A("ra", [128, 4, 64], F32)
    rb = A("rb", [128, 4, 64], F32)
    krd = A("krd", [128, 4, 128], BF16)
    NG_KV = ng_kv
    NT = 4 * NG_KV

    def ld_x(t):
        S.dma(lambda e: e.dma_start(out=xt[t % NXT][:], in_=D["x_kv"][t * 128:(t + 1) * 128, :]), sl_x[t % NXT],
              (), [xt[t % NXT]])

    def f_sc(t):
        G, tq, tp = t // 4, t % 4, t % 2
        gp = G % 2
        sqj = sqjs[tp]
        xt_ = xt[t % NXT]
        act(sqj[:], xt_[:], AF.Square, [xt_], [sqj, ssqx[gp]], accum=ssqx[gp][:, tq:tq + 1])
        cp("dve", xb[tp][:], xt_[:], [xt_], [xb[tp]])
        if t + NXT < NT:
            ld_x(t + NXT)

    def f_te(t):
        G, tq, tp = t // 4, t % 4, t % 2
        gp = G % 2
        ptr = bkb(tp).rearrange("p (c t) -> p c t", c=8)
        for c in range(8):
            tr(ptr[:, c, :], xb[tp][:, c * 128:(c + 1) * 128], idb[:], [xb[tp], idb], [bk[tp]], sig=(c == 7))
        cp("dve", xTg[gp][:, :, tq * 128:(tq + 1) * 128], ptr, [bk[tp]], [xTg[gp]])

    def kv_front(G):
        for tq in range(4):
            t = 4 * G + tq
            if t + 1 < NT:
                f_sc(t + 1)
            yield
            f_te(t)
            yield

    def kv_back_a(G):
        gp = G % 2
        cst = csts[gp]
        ssqp = ssqps[gp]
        for tq in range(4):
            pcv = bk[2 + tq // 2].h[:, (tq % 2) * 192:(tq % 2) * 192 + 192]
            for c in range(8):
                mm(pcv, xTg[gp][:, c, tq * 128:(tq + 1) * 128], w_kv[:, c, :], c == 0, c == 7,
                   [xTg[gp], w_kv], [bk[2 + tq // 2]], sig=(c == 7))
            if tq % 2:
                yield
        rstd_from(ssqx[gp][:], rstdx[gp][:], 1024, [ssqx[gp]], [rstdx[gp]])
        yield
        for tq in range(4):
            pcv = bk[2 + tq // 2].h[:, (tq % 2) * 192:(tq % 2) * 192 + 192]
            act(cst[:, tq, :], pcv, AF.Copy, [bk[2 + tq // 2], rstdx[gp]], [cst], scale=rstdx[gp][:, tq:tq + 1])
            if tq % 2:
                yield
        cstf = cst[:].rearrange("p a b -> p (a b)")
        tt("dve", scr[:], cstf, cstf, ALU.mult, [cst], [scr])
        yield
        scr3 = scr[:].rearrange("p (a b) -> p a b", a=4)
        red(ssqc[:], scr3[:, :, 0:128], [scr], [ssqc])
        red(ssqp[:], scr3[:, :, 128:192], [scr], [ssqp])
        yield
        rstd_from(ssqc[:], rstdc[:], 128, [ssqc], [rstdc])
        yield
        tt("dve", vtmp[:], cst[:, :, 0:128], bc(rstdc[:], 2, 128), ALU.mult, [cst, rstdc], [vtmp])
        yield
        tt("dve", Vp[:, 4 * G:4 * G + 4, :], vtmp[:], bc(g_ckv_b[:], 1, 4), ALU.mult, [vtmp, g_ckv_b], [Vp])
        yield
        ptv = bkb(6).rearrange("p (c t) -> p c t", c=8)
        for tq in range(4):
            tr(ptv[:, tq, :], Vp[:, 4 * G + tq, :], idb[:], [Vp, idb], [bk[6]], sig=(tq == 3))
        yield
        cp("act", KcT[:, G * 512:(G + 1) * 512].rearrange("p (a b) -> p a b", a=4), ptv[:, 0:4, :], [bk[6]], [KcT])
        job_consume()

    def kv_back_b(G):
        gp = G % 2
        cst = csts[gp]
        ssqp = ssqps[gp]
        tt("pool", krg[:], cst[:, :, 128:192], bc(g_kr_b[:], 1, 4), ALU.mult, [cst, g_kr_b], [krg])
        cK = cosK[:, 4 * G:4 * G + 4, :]
        sK = sinK[:, 4 * G:4 * G + 4, :]
        tt("pool", ra[:, :, 0:32], krg[:, :, 0:32], cK, ALU.mult, [krg, cosK], [ra])
        tt("pool", ra[:, :, 32:64], krg[:, :, 32:64], cK, ALU.mult, [krg, cosK], [ra])
        yield
        for tq in range(4):
            pb = bk[4 + tq % 2]
            mm(pb.h[:, :], KcT[:, (4 * G + tq) * 128:(4 * G + tq + 1) * 128], w_uk[:].rearrange("p h n -> p (h n)"),
               True, True, [KcT, w_uk], [pb])
            if tq == 0:
                tt("pool", rb[:, :, 0:32], krg[:, :, 32:64], sK, ALU.mult, [krg, sinK], [rb])
                tt("pool", rb[:, :, 32:64], krg[:, :, 0:32], sK, ALU.mult, [krg, sinK], [rb])
            if tq == 1:
                tt("pool", krd[:, :, 0:32], ra[:, :, 0:32], rb[:, :, 0:32], ALU.subtract, [ra, rb], [krd])
                tt("pool", krd[:, :, 32:64], ra[:, :, 32:64], rb[:, :, 32:64], ALU.add, [ra, rb], [krd])
            if tq == 2:
                cp("pool", krd[:, :, 64:128], krd[:, :, 0:64], [krd], [krd])
            yield
            act(sqk[:, tq, :], pb.h[:, :], AF.Square, [pb], [sqk])
            yield
        red(ssqkn[:].rearrange("p a b -> p (a b)"), sqk[:].rearrange("p a (h n) -> p (a h) n", h=4), [sqk], [ssqkn])
        yield
        tt("dve", ssqkn[:], ssqkn[:], bc(ssqp[:], 2, 4), ALU.add, [ssqkn, ssqp], [ssqkn])
        yield
        rk = rstdk[:, 4 * G:4 * G + 4, :]
        act(rk, ssqkn[:], AF.Ln, [ssqkn, epsb], [rstdk], scale=1.0 / 192, bias=epsb[:])
        act(rk, rk, AF.Exp, [rstdk, b192], [rstdk], scale=-0.5, bias=b192[:])
        yield
        ptk = bkb(7).rearrange("p (c t) -> p c t", c=8)
        for tq in range(4):
            tr(ptk[:, tq, :], krd[:, tq, :], idb[:], [krd, idb], [bk[7]], sig=(tq == 3))
        yield
        hp = 0 if G < 8 else 64
        col0 = (G % 8) * 512
        cp("act", KrT[hp:hp + 64, col0:col0 + 512].rearrange("p (a b) -> p a b", a=4), ptk[hp:hp + 64, 0:4, :],
           [bk[7]], [KrT])
        job_consume()

    job_issue()
    job_issue()
    f_sc(0)
    for r in range(NG_KV + 2):
        gens = []
        if r < NG_KV:
            gens.append(kv_front(r))
        if 0 <= r - 1 < NG_KV:
            gens.append(kv_back_a(r - 1))
        if 0 <= r - 2 < NG_KV:
            gens.append(kv_back_b(r - 2))
        run_pipelined(gens, skew=0, maxlive=3)
    while job_state["done"] < len(job_list):
        job_consume()

    S.barrier()
    if limit == "kv":
        S.emit()
        return nc, AR.peak
    AR.release(m_regionB)

    w_o = A("w_o", [128, 8, 1024], BF16)
    w_plg = A("w_plg", [128, 8, 1024], BF16)
    w_pl = A("w_pl", [128, 2, 1024], BF16)
    wepi_jobs = []
    for (src, dstT, nchunk, gain) in ((D["w_o"], w_o, 8, g_o_pp), (D["w_plg"], w_plg, 8, g_pl_pp),
                                      (D["w_pl"], w_pl, 2, None)):
        for c in range(nchunk):
            wepi_jobs.append((src, dstT, c, gain))
    wepi_state = dict(issued=0, done=0, bufs=None, slots=[S.slot("w2_%d" % i) for i in range(4)])

    def wepi_issue():
        k = wepi_state["issued"]
        if k >= len(wepi_jobs):
            return
        src, dstT, c, gain = wepi_jobs[k]
        ws = wepi_state["bufs"][k % 4]
        S.dma(lambda e: e.dma_start(out=ws[:], in_=src[c * 128:(c + 1) * 128, :]), wepi_state["slots"][k % 4], (), [ws])
        wepi_state["issued"] = k + 1

    def wepi_consume():
        k = wepi_state["done"]
        if k >= len(wepi_jobs):
            return
        src, dstT, c, gain = wepi_jobs[k]
        ws = wepi_state["bufs"][k % 4]
        if gain is None:
            cp("dve", dstT[:, c, :], ws[:], [ws], [dstT])
        elif k % 2:
            act(dstT[:, c, :], ws[:], AF.Copy, [ws, gain], [dstT], scale=gain[:, c:c + 1])
        else:
            ts("dve", dstT[:, c, :], ws[:], gain[:, c:c + 1], None, ALU.mult, None, [ws, gain], [dstT])
        wepi_state["done"] = k + 1
        wepi_issue()

    if limit == "wepi":
        return finish()

    sl_xo = [S.slot("xo0"), S.slot("xo1")]
    sl_xr = [S.slot("xr0"), S.slot("xr1")]
    outst = [A("outst0", [128, 1024], F32), A("outst1", [128, 1024], F32)]
    out_toks = []

    for m in range(n_m):
        mA = AR.mark()
        xTh = A("xTh", [128, 8, 4, 130], BF16)

        mL1 = AR.mark()
        xos = [A("xo0", [128, 1024], F32), A("xo1", [128, 1024], F32)]
        xobs = [A("xob0", [128, 1024], BF16), A("xob1", [128, 1024], BF16)]
        sjs = [A("sj0", [128, 1024], BF16), A("sj1", [128, 1024], BF16)]
        sos = [A("so0", [128, 1], F32), A("so1", [128, 1], F32)]

        def l1_tile(s):
            t = 4 * m + s
            xo, xob, sj, so = xos[s % 2], xobs[s % 2], sjs[s % 2], sos[s % 2]
            S.dma(lambda e: e.dma_start(out=xo[:], in_=D["x_own"][t]), sl_xo[s % 2], (), [xo])
            yield
            act(sj[:], xo[:], AF.Square, [xo], [sj, so], accum=so[:])
            yield
            rstd_from(so[:], so[:], 1024, [so], [so])
            yield
            ts("dve", xob[:], xo[:], so[:], None, ALU.mult, None, [xo, so], [xob])
            yield
            ptr = bkb(s % 2).rearrange("p (c t) -> p c t", c=8)
            for c in range(8):
                tr(ptr[:, c, :], xob[:, c * 128:(c + 1) * 128], idb[:], [xob, idb], [bk[s % 2]], sig=(c == 7))
            yield
            cp("dve", xTh[:, :, s, 2:130], ptr, [bk[s % 2]], [xTh])

        run_pipelined([l1_tile(s) for s in range(4)], skew=1, maxlive=2)
        cp("pool", xTh[:, :, :, 0:2], xThalo[:, :, 4 * m:4 * m + 4, :], [xThalo], [xTh])
        AR.release(mL1)
        if m == 0:
            rope_tables(poso_i, 16, cosQ, sinQ)
        if limit == "A1a":
            return finish()

        mQ = AR.mark()
        cqnT = A("cqnT", [128, 2, 512], BF16)
        qnT = A("qnT", [128, 4, 512], BF16)
        mQ1 = AR.mark()
        cq_sb = A("cq_sb", [128, 4, 256], F32)
        scq = A("scq", [128, 1024], F32)
        s4 = A("s4", [128, 4], F32)
        r4 = A("r4", [128, 4], F32)
        cqn = A("cqn", [128, 4, 256], BF16)
        ezs = [A("ez0", [128, 512], F32), A("ez1", [128, 512], F32)]

        def cq_chain():
            for s in range(4):
                pv = bk[2 + s // 2].h[:, (s % 2) * 256:(s % 2) * 256 + 256]
                for c in range(8):
                    mm(pv, xTh[:, c, s, 2:130], w_own[:, c, 0:256], c == 0, c == 7, [xTh, wdst_buf],
                       [bk[2 + s // 2]], sig=(c == 7))
                yield
            for b2 in range(2):
                cp("act", cq_sb[:, 2 * b2:2 * b2 + 2, :].rearrange("p a b -> p (a b)"), bk[2 + b2].h[:, :],
                   [bk[2 + b2]], [cq_sb])
            yield
            cqf = cq_sb[:].rearrange("p a b -> p (a b)")
            tt("dve", scq[:], cqf, cqf, ALU.mult, [cq_sb], [scq])
            red(s4[:], scq[:].rearrange("p (a b) -> p a b", a=4), [scq], [s4])
            yield
            rstd_from(s4[:], r4[:], 256, [s4], [r4])
            yield
            tt("dve", cqn[:], cq_sb[:], bc(r4[:], 2, 256), ALU.mult, [cq_sb, r4], [cqn])
            yield
            ptq = bkb(4).rearrange("p (c s t) -> p c s t", c=2, s=4)
            for s in range(4):
                for c2 in range(2):
                    tr(ptq[:, c2, s, :], cqn[:, s, c2 * 128:(c2 + 1) * 128], idb[:], [cqn, idb], [bk[4]],
                       sig=(s == 3 and c2 == 1))
            yield
            cp("dve", cqnT[:].rearrange("p c t -> p (c t)"), bkb(4), [bk[4]], [cqnT])

        def za_tile(s):
            pz = bk[5 + s % 2]
            ez = ezs[s % 2]
            for c in range(8):
                mm(pz.h[:, :], xTh[:, c, s, 2:130], w_own[:, c, 256:768], c == 0, c == 7, [xTh, wdst_buf], [pz],
                   sig=(c == 7))
            yield
            sigmoid_act(ez[:], pz.h[:, :], [pz], [ez])
            yield
            tt("dve", sza[:, s, :], pz.h[:, :], ez[:], ALU.mult, [pz, ez], [sza])

        run_pipelined([cq_chain()] + [za_tile(s) for s in range(4)], skew=2, maxlive=3)
        AR.release(mQ1)
        if limit == "A1b":
            return finish()

        q_sbs = [A("q_sb0", [128, 4, 192], F32), A("q_sb1", [128, 4, 192], F32)]
        sq3 = A("sq3", [128, 768], F32)
        s4s = [A("s4a", [128, 4], F32), A("s4b", [128, 4], F32)]
        r4s = [A("r4a", [128, 4], F32), A("r4b", [128, 4], F32)]
        qnbs = [A("qnb0", [128, 4, 128], BF16), A("qnb1", [128, 4, 128], BF16)]
        qrs = [A("qr0", [128, 4, 64], F32), A("qr1", [128, 4, 64], F32)]
        qa = A("qa", [128, 4, 64], F32)
        qb = A("qb", [128, 4, 64], F32)
        qrbs = [A("qrb0", [128, 4, 128], BF16), A("qrb1", [128, 4, 128], BF16)]

        def q_tile(s):
            t = 4 * m + s
            par = s % 2
            pa, pb = bk[2 * par], bk[2 * par + 1]
            q_sb, s4_, r4_, qnb, qr, qrb = q_sbs[par], s4s[par], r4s[par], qnbs[par], qrs[par], qrbs[par]
            for c2 in range(2):
                mm(pa.h[:, :], cqnT[:, c2, s * 128:(s + 1) * 128], w_uq[:, c2, 0:512], c2 == 0, c2 == 1,
                   [cqnT, wdst_buf], [pa], sig=False)
            for c2 in range(2):
                mm(pb.h[:, 0:256], cqnT[:, c2, s * 128:(s + 1) * 128], w_uq[:, c2, 512:768], c2 == 0, c2 == 1,
                   [cqnT, wdst_buf], [pb], sig=(c2 == 1))
            yield
            qf = q_sb[:].rearrange("p a b -> p (a b)")
            cp("act", qf[:, 0:512], pa.h[:, :], [pa], [q_sb])
            cp("act", qf[:, 512:768], pb.h[:, 0:256], [pb], [q_sb])
            yield
            tt("dve", sq3[:], qf, qf, ALU.mult, [q_sb], [sq3])
            red(s4_[:], sq3[:].rearrange("p (a b) -> p a b", a=4), [sq3], [s4_])
            yield
            rstd_from(s4_[:], r4_[:], 192, [s4_], [r4_])
            yield
            tt("dve", qnb[:], q_sb[:, :, 0:128], bc(r4_[:], 2, 128), ALU.mult, [q_sb, r4_], [qnb])
            tt("dve", qr[:], q_sb[:, :, 128:192], bc(r4_[:], 2, 64), ALU.mult, [q_sb, r4_], [qr])
            yield
            tt("dve", qr[:], qr[:], bc(g_qr_b[:], 1, 4), ALU.mult, [qr, g_qr_b], [qr])
            cQ = bc(cosQ[:, t, :], 1, 4)
            sQ = bc(sinQ[:, t, :], 1, 4)
            tt("dve", qa[:, :, 0:32], qr[:, :, 0:32], cQ, ALU.mult, [qr, cosQ], [qa])
            tt("dve", qa[:, :, 32:64], qr[:, :, 32:64], cQ, ALU.mult, [qr, cosQ], [qa])
            yield
            tt("dve", qb[:, :, 0:32], qr[:, :, 32:64], sQ, ALU.mult, [qr, sinQ], [qb])
            tt("dve", qb[:, :, 32:64], qr[:, :, 0:32], sQ, ALU.mult, [qr, sinQ], [qb])
            yield
            tt("dve", qrb[:, :, 0:32], qa[:, :, 0:32], qb[:, :, 0:32], ALU.subtract, [qa, qb], [qrb])
            tt("dve", qrb[:, :, 32:64], qa[:, :, 32:64], qb[:, :, 32:64], ALU.add, [qa, qb], [qrb])
            cp("dve", qrb[:, :, 64:128], qrb[:, :, 0:64], [qrb], [qrb])
            yield
            ptn = bkb(4 + par).rearrange("p (c t) -> p c t", c=8)
            for h in range(4):
                tr(ptn[:, h, :], qnb[:, h, :], idb[:], [qnb, idb], [bk[4 + par]], sig=False)
            for h in range(4):
                tr(ptn[:, 4 + h, :], qrb[:, h, :], idb[:], [qrb, idb], [bk[4 + par]], sig=(h == 3))
            yield
            cp("dve", qnT[:, :, s * 128:(s + 1) * 128], ptn[:, 0:4, :], [bk[4 + par]], [qnT])
            cp("act", QrT[0:64, 0, :, s * 128:(s + 1) * 128], ptn[0:64, 4:8, :], [bk[4 + par]], [QrT])
            cp("act", QrT[64:128, 1, :, s * 128:(s + 1) * 128], ptn[64:128, 4:8, :], [bk[4 + par]], [QrT])

        run_pipelined([q_tile(s) for s in range(4)], skew=4, maxlive=2)
        for h in range(4):
            pb_ = bk[6 + h % 2]
            mm(pb_.h[:, :], Aabs[:, h, :], qnT[:, h, :], True, True, [Aabs, qnT], [pb_])
            cp("act" if h % 2 else "dve", QpT[:, h, :], pb_.h[:, :], [pb_], [QpT])
        AR.release(mQ)
        if limit == "A1":
            return finish()

        ccss = [A("ccs0", [128, 2, 130], F32), A("ccs1", [128, 2, 130], F32)]
        prods = [A("prod0", [128, 4, 130], F32), A("prod1", [128, 4, 130], F32)]
        uus = [A("uu0", [128, 4, 128], F32), A("uu1", [128, 4, 128], F32)]
        e2s = [A("e20", [128, 512], F32), A("e21", [128, 512], F32)]
        g1s = [A("g10", [128, 512], F32), A("g11", [128, 512], F32)]
        gT = A("gT", [128, 4, 512], F32)
        sqb4 = A("sqb4", [128, 4, 512], BF16)
        rbc = A("rbc", [128, 512], F32)
        CB, CC, CX, ZC = 768, 1280, 1792, 2304

        def conv_chunk(j):
            par = j % 2
            ccs, prod, uu, e2, g1 = ccss[par], prods[par], uus[par], e2s[par], g1s[par]
            pcc, pcx, pcb, pzc = bk[4 * par], bk[4 * par + 1], bk[4 * par + 2], bk[4 * par + 3]
            for tp in range(2):
                for c in range(8):
                    mm(pcc.h[:, 0:260].rearrange("p (a b) -> p a b", a=2), w_own[:, c, CC + j * 128:CC + (j + 1) * 128],
                       xTh[:, c, 2 * tp:2 * tp + 2, :], c == 0, c == 7, [xTh, wdst_buf], [pcc], sig=(c == 7))
                for c in range(8):
                    mm(pcx.h[:, 0:260].rearrange("p (a b) -> p a b", a=2), w_own[:, c, CX + j * 128:CX + (j + 1) * 128],
                       xTh[:, c, 2 * tp:2 * tp + 2, :], c == 0, c == 7, [xTh, wdst_buf], [pcx], sig=(c == 7))
                yield
                cp("act", ccs[:].rearrange("p a b -> p (a b)"), pcc.h[:, 0:260], [pcc], [ccs])
                yield
                tt("dve", prod[:, 2 * tp:2 * tp + 2, :].rearrange("p a b -> p (a b)"),
                   ccs[:].rearrange("p a b -> p (a b)"), pcx.h[:, 0:260], ALU.mult, [ccs, pcx], [prod])
                yield
            for c in range(8):
                mm(pcb.h[:, :].rearrange("p (a b) -> p a b", a=4), w_own[:, c, CB + j * 128:CB + (j + 1) * 128],
                   xTh[:, c, :, 2:130], c == 0, c == 7, [xTh, wdst_buf], [pcb], sig=(c == 7))
            for c in range(8):
                mm(pzc.h[:, :].rearrange("p (a b) -> p a b", a=4), w_own[:, c, ZC + j * 128:ZC + (j + 1) * 128],
                   xTh[:, c, :, 2:130], c == 0, c == 7, [xTh, wdst_buf], [pzc], sig=(c == 7))
            ts("dve", uu[:], prod[:, :, 0:128], cw[:, j, 0:1], None, ALU.mult, None, [prod, cw], [uu])
            yield
            stt("dve", uu[:], prod[:, :, 1:129], cw[:, j, 1:2], uu[:], ALU.mult, ALU.add, [prod, cw, uu], [uu])
            sigmoid_act(e2[:], pzc.h[:, :], [pzc], [e2])
            yield
            stt("dve", uu[:], prod[:, :, 2:130], cw[:, j, 2:3], uu[:], ALU.mult, ALU.add, [prod, cw, uu], [uu])
            yield
            tt("dve", e2[:], pzc.h[:, :], e2[:], ALU.mult, [pzc, e2], [e2])
            tt("dve", g1[:], pcb.h[:, :], uu[:].rearrange("p a b -> p (a b)"), ALU.mult, [pcb, uu], [g1])
            yield
            tt("dve", gT[:, j, :], g1[:], e2[:], ALU.mult, [g1, e2], [gT])
            yield
            act(sqb4[:, j, :], gT[:, j, :], AF.Square, [gT], [sqb4])

        run_pipelined([conv_chunk(j) for j in range(4)], skew=5, maxlive=2)
        for j in range(4):
            mm(bk[0].h[:, :], ones_bf[:], sqb4[:, j, :], j == 0, j == 3, [ones_bf, sqb4], [bk[0]], sig=(j == 3))
        act(rbc[:], bk[0].h[:, :], AF.Ln, [bk[0], epsb], [rbc], scale=1.0 / 512, bias=epsb[:])
        act(rbc[:], rbc[:], AF.Exp, [rbc, zerob], [rbc], scale=-0.5, bias=zerob[:])
        for j in range(4):
            tt("dve", ycT[:, j, :], gT[:, j, :], rbc[:], ALU.mult, [gT, rbc], [ycT])
        AR.release(mA)
        if limit == "A2":
            return finish()

        mAt = AR.mark()
        NPT = 6
        OTb = A("OTb", [128, 4, 512], BF16)
        rsum = A("rsum", [128, 4, 4], F32)
        mPT = AR.mark()
        pt = [A("pt%d" % i, [128, 512], BF16) for i in range(NPT)]
        racc = [A("racc0", [128, 512], F32), A("racc1", [128, 512], F32)]
        if m == 0:
            wepi_state["bufs"] = [A("wst2_%d" % i, [128, 1024], F32) for i in range(4)]
            for _ in range(4):
                wepi_issue()
        nk = 16 * m + 16
        it = 0
        def finish_head(hh):
            cp("dve", OTb[:, hh, :], bk[4 + hh % 2].h[:, :], [bk[4 + hh % 2]], [OTb])
            for s_ in range(4):
                mm(bk[6].h[:, s_ * 4 + hh:s_ * 4 + hh + 1], racc[hh % 2][:, s_ * 128:(s_ + 1) * 128], ones_f[:],
                   True, True, [racc[hh % 2], ones_f], [bk[6]], sig=(s_ == 3))

        for h in range(4):
            hp = (h % 2) * 64
            po = bk[4 + h % 2]
            rc = racc[h % 2]

            def s_mm(kb, it_):
                r = kb - 16 * m
                s0 = 0 if r < 0 else r // 4
                c0 = s0 * 128
                ps = bk[it_ % 4]
                kp = 0 if kb < 32 else 64
                kc = (kb % 32) * 128
                mm(ps.h[:, c0:512], KcT[:, kb * 128:(kb + 1) * 128], QpT[:, h, c0:512], True, False, [QpT], [ps],
                   sig=False)
                mm(ps.h[:, c0:512], KrT[:, kc:kc + 128], QrT[:, kp // 64, h, c0:512], False, r < 0,
                   [QrT], [ps], sig=(r < 0))
                if r >= 0:
                    midx = (s0 % 2) * 4 + (r % 4)
                    mm(ps.h[:, c0:c0 + 128], idb[:], maskb[:, midx, :], False, True, [idb, maskb], [ps])
                return c0

            def c0_of(kb):
                r = kb - 16 * m
                return 0 if r < 0 else (r // 4) * 128

            s_mm(0, it)
            if nk > 1:
                s_mm(1, it + 1)
            for kb in range(nk):
                c0 = c0_of(kb)
                ps = bk[it % 4]
                if kb + 2 < nk:
                    s_mm(kb + 2, it + 2)
                ptb = pt[it % NPT]
                act(ptb[:, c0:512], ps.h[:, c0:512], AF.Exp, [ps, zerob], [ptb], scale=rstdk[:, kb, h:h + 1],
                    bias=zerob[:])
                if kb == 0:
                    cp("dve", rc[:], ptb[:], [ptb], [rc])
                else:
                    tt("dve", rc[:, c0:512], rc[:, c0:512], ptb[:, c0:512], ALU.add, [rc, ptb], [rc])
                mm(po.h[:, c0:512], Vp[:, kb, :], ptb[:, c0:512], kb == 0, kb == nk - 1, [ptb], [po],
                   sig=(kb == nk - 1))
                it += 1
                if h > 0 and kb == min(5, nk - 1):
                    finish_head(h - 1)
                if m == 0 and kb % 3 == 2:
                    wepi_consume()
        finish_head(3)
        if m == 0:
            while wepi_state["done"] < len(wepi_jobs):
                wepi_consume()
        cp("dve", rsum[:].rearrange("p a b -> p (a b)"), bk[6].h[:, 0:16], [bk[6]], [rsum])
        recip(rsum[:], rsum[:], [rsum], [rsum])

        if limit == "att":
            return finish()
        AR.release(mPT)
        o1 = A("o1", [128, 512], F32)
        sg1 = A("sg1", [128, 1], F32)
        sg2 = A("sg2", [128, 1], F32)
        sjk = A("sjk", [128, 1024], BF16)
        yb = A("yb", [128, 512], BF16)
        yaT = A("yaT", [128, 4, 128], BF16)
        xrs = [A("xr0", [128, 1024], F32), A("xr1", [128, 1024], F32)]
        x1s = [A("x10", [128, 1024], F32), A("x11", [128, 1024], F32)]
        x1b = A("x1b", [128, 1024], BF16)
        x1T = A("x1T", [128, 8, 128], BF16)
        gt = A("gt", [128, 1024], F32)
        pfs = [A("pf0", [128, 256], F32), A("pf1", [128, 256], F32)]
        pb16 = A("pb16", [128, 256], BF16)
        pTt = A("pTt", [128, 2, 128], BF16)

        def epi_tile(s):
            t = 4 * m + s
            xr, x1, pf = xrs[s % 2], x1s[s % 2], pfs[s % 2]
            S.dma(lambda e: e.dma_start(out=xr[:], in_=D["x_own"][t]), sl_xr[s % 2], (), [xr])
            S.dma(lambda e: e.dma_start(out=pf[:], in_=D["p_own"][t]), sl_p[s % 2], (), [pf])
            yield
            pov = bk[6]
            for h in range(4):
                mm(pov.h[:, h * 128:(h + 1) * 128], OTb[:, h, s * 128:(s + 1) * 128], w_uv[:, h, :], True, True,
                   [OTb, w_uv], [pov], sig=(h == 3))
            yield
            tt("dve", o1[:].rearrange("p (a b) -> p a b", a=4), pov.h[:, :].rearrange("p (a b) -> p a b", a=4),
               bc(rsum[:, s, :], 2, 128), ALU.mult, [pov, rsum], [o1])
            tt("dve", o1[:], o1[:], sza[:, s, :], ALU.mult, [o1, sza], [o1])
            yield
            act(sjk[:, 0:512], o1[:], AF.Square, [o1], [sjk, sg1], accum=sg1[:])
            rstd_from(sg1[:], sg1[:], 512, [sg1], [sg1])
            yield
            ts("dve", yb[:], o1[:], sg1[:], None, ALU.mult, None, [o1, sg1], [yb])
            yield
            pty = bkb(7).rearrange("p (c t) -> p c t", c=8)
            for c in range(4):
                tr(pty[:, c, :], yb[:, c * 128:(c + 1) * 128], idb[:], [yb, idb], [bk[7]], sig=(c == 3))
            cp("dve", yaT[:], pty[:, 0:4, :], [bk[7]], [yaT])
            yield
            if WARM:
                pe_warm(bk[0], w_o[:, 0, 0:512], w_o, WARM)
            for n in range(2):
                pz = bk[n]
                for c in range(8):
                    lhs = yaT[:, c, :] if c < 4 else ycT[:, c - 4, s * 128:(s + 1) * 128]
                    mm(pz.h[:, :], lhs, w_o[:, c, n * 512:(n + 1) * 512], c == 0, c == 7, [yaT, ycT, w_o], [pz],
                       sig=(c == 7))
                tt("dve", x1[:, n * 512:(n + 1) * 512], pz.h[:, :], xr[:, n * 512:(n + 1) * 512], ALU.add,
                   [pz, xr], [x1])
                yield
            act(sjk[:], x1[:], AF.Square, [x1], [sjk, sg2], accum=sg2[:])
            rstd_from(sg2[:], sg2[:], 1024, [sg2], [sg2])
            yield
            ts("dve", x1b[:], x1[:], sg2[:], None, ALU.mult, None, [x1, sg2], [x1b])
            cp("dve", pb16[:], pf[:], [pf], [pb16])
            yield
            ptx = bkb(2).rearrange("p (c t) -> p c t", c=8)
            for c in range(8):
                tr(ptx[:, c, :], x1b[:, c * 128:(c + 1) * 128], idb[:], [x1b, idb], [bk[2]], sig=(c == 7))
            cp("dve", x1T[:], ptx, [bk[2]], [x1T])
            yield
            ptp = bkb(5).rearrange("p (c t) -> p c t", c=8)
            for c in range(2):
                tr(ptp[:, 4 + c, :], pb16[:, c * 128:(c + 1) * 128], idb[:], [pb16, idb], [bk[5]], sig=(c == 1))
            cp("dve", pTt[:], ptp[:, 4:6, :], [bk[5]], [pTt])
            yield
            ost = outst[t % 2]
            if WARM:
                pe_warm(bk[3], w_o[:, 0, 0:512], w_o, WARM)
            for n in range(2):
                pg = bk[3 + n]
                for c in range(8):
                    mm(pg.h[:, :], x1T[:, c, :], w_plg[:, c, n * 512:(n + 1) * 512], c == 0, c == 7, [x1T, w_plg],
                       [pg], sig=(c == 7))
                gts = gt[:, n * 512:(n + 1) * 512]
                sigmoid_act(gts, pg.h[:, :], [pg], [gt])
                yield
                pp = bk[5]
                for c in range(2):
                    mm(pp.h[:, :], pTt[:, c, :], w_pl[:, c, n * 512:(n + 1) * 512], c == 0, c == 1, [pTt, w_pl], [pp],
                       sig=(c == 1))
                yield
                tt("dve", gts, pp.h[:, :], gts, ALU.mult, [pp, gt], [gt])
                tt("dve", ost[:, n * 512:(n + 1) * 512], gts, x1[:, n * 512:(n + 1) * 512], ALU.add, [gt, x1], [ost])
                yield
            tok = S.dma(lambda e: e.dma_start(out=out[t], in_=ost[:]), sl_out[t % 2], [ost], ())
            out_toks.append(tok)

        run_pipelined([epi_tile(s) for s in range(4)], skew=8, maxlive=2)
        AR.release(mAt)

    S.wait_all("sp", out_toks[-2:])
    S.emit()
    return nc, AR.peak


_CACHE = {}


def _get_program():
    if "nc" not in _CACHE:
        _CACHE["nc"] = build_program()[0]
    return _CACHE["nc"]


def _host_inputs(x, p, positions, g_in, w_in, g_cq, w_uq, g_ckv, w_ukv, g_q, g_k, conv_w, g_oa, g_oc, w_o, w_pl,
                 w_plg, g_pl):
    f = lambda a: np.ascontiguousarray(np.asarray(a, dtype=np.float32))
    x = f(x)
    p = f(p)
    positions = np.ascontiguousarray(np.asarray(positions, dtype=np.int32))
    inv_freq = (1.0 / (10000.0 ** (np.arange(0, 64, 2, dtype=np.float32) / np.float32(64)))).astype(np.float32)
    tri = np.where(np.arange(128)[:, None] <= np.arange(128)[None, :], 0.0, NEG).astype(np.float32)
    full = np.full((128, 128), NEG, np.float32)
    zero = np.zeros((128, 128), np.float32)
    shared = dict(
        ident=np.eye(128, dtype=np.float32),
        inv_freq=inv_freq,
        g_in_pp=f(f(g_in)[0].reshape(8, 128).T),
        g_pl_pp=f(f(g_pl)[0].reshape(8, 128).T),
        g_o_pp=f(np.concatenate([f(g_oa)[0], f(g_oc)[0]]).reshape(8, 128).T),
        g_cq_pp=f(f(g_cq)[0].reshape(2, 128).T),
        conv_wT=f(f(conv_w)[0].reshape(3, 4, 128).transpose(2, 1, 0)),
        w_in=f(w_in)[0], w_uq=f(w_uq)[0], g_ckv=f(g_ckv)[0], w_ukv=f(w_ukv)[0], g_q=f(g_q)[0], g_k=f(g_k)[0],
        w_o=f(w_o)[0], w_pl=f(w_pl)[0], w_plg=f(w_plg)[0],
    )
    in_maps, blocks_all = [], []
    for core in range(8):
        b, j = core // 4, core % 4
        blocks = [16 * m + o for m in range(4) for o in (j, 7 - j, 8 + j, 15 - j)]
        blocks_all.append(blocks)
        xb_ = x[b].reshape(64, 128, 1024)
        halo = np.zeros((16, 2, 1024), np.float32)
        for i, blk in enumerate(blocks):
            if blk > 0:
                halo[i] = x[b, blk * 128 - 2:blk * 128]
        pos_b = positions[b].reshape(64, 128)

        def mk(r, target):
            return zero if r < target else (tri if r == target else full)
        masks = np.stack([mk(r, j) for r in range(4)] + [mk(r, 3 - j) for r in range(4)], axis=1)
        d = dict(shared)
        d.update(
            x_kv=x[b],
            x_own=f(xb_[blocks]),
            x_halo=f(halo.reshape(32, 1024)),
            p_own=f(p[0, b].reshape(64, 128, 256)[blocks]),
            pos_kv=np.ascontiguousarray(pos_b.T),
            pos_own=np.ascontiguousarray(pos_b[blocks].T),
            masks=f(masks),
        )
        in_maps.append(d)
    return in_maps, blocks_all


def kernel(**inputs):
    in_maps, blocks_all = _host_inputs(**inputs)
    nc = _get_program()
    res = run_bass_kernel_spmd(nc, in_maps, core_ids=list(range(8)))
    out = np.empty((2, 64, 128, 1024), np.float32)
    for core in range(8):
        o = np.asarray(res.results[core]["out_own"]).reshape(16, 128, 1024)
        out[core // 4, blocks_all[core]] = o
    return out.reshape(2, 8192, 1024)
```

```python
import math
import numpy as np
import concourse.bass as bass
import concourse.mybir as mybir
from concourse.bass_utils import run_bass_kernel_spmd

F32 = mybir.dt.float32
BF16 = mybir.dt.bfloat16
I32 = mybir.dt.int32
AF = mybir.ActivationFunctionType
ALU = mybir.AluOpType
AX = mybir.AxisListType

NEG = -30000.0
WARM = 5
EPS = 1e-6


class Buf:
    __slots__ = ("name", "w", "r", "excl")

    def __init__(self, name, excl=False):
        self.name = name
        self.w = None
        self.r = {}
        self.excl = excl


class T:
    __slots__ = ("h", "b")

    def __init__(self, h, b):
        self.h = h
        self.b = b

    def __getitem__(self, k):
        return self.h[k]


def _b(x):
    return x.b if isinstance(x, T) else x


class Sched:
    ENG = ("pe", "act", "dve", "pool", "sp")

    def __init__(self, nc):
        self.nc = nc
        self.sem = {e: nc.alloc_semaphore("s_" + e) for e in self.ENG}
        self.cnt = {e: 0 for e in self.ENG}
        self.seen = {e: {} for e in self.ENG}
        self.prog = {e: [] for e in self.ENG}
        self.pend = {e: False for e in self.ENG}
        self.slots = []

    def slot(self, name):
        s = dict(sem=self.nc.alloc_semaphore("d_" + name), cnt=0)
        self.slots.append(s)
        return s

    def _deps(self, e, reads, writes):
        deps = {}

        def add(s, v):
            if deps.get(s, 0) < v:
                deps[s] = v
        own = self.sem[e]
        for b in reads:
            b = _b(b)
            if b.w is not None:
                add(*b.w)
            if b.excl:
                for s, v in b.r.items():
                    if s is not own:
                        add(s, v)
        skip_own = (e == "pe")
        for b in writes:
            b = _b(b)
            if b.w is not None and not (skip_own and b.w[0] is own):
                add(*b.w)
            for s, v in b.r.items():
                if not (skip_own and s is own):
                    add(s, v)
        out = []
        for s, v in deps.items():
            if self.seen[e].get(s, 0) < v:
                self.seen[e][s] = v
                out.append((s, v))
        return out

    def _mark(self, tok, reads, writes):
        for b in reads:
            b = _b(b)
            if b.r.get(tok[0], 0) < tok[1]:
                b.r[tok[0]] = tok[1]
        for b in writes:
            b = _b(b)
            b.w = tok
            b.r = {}

    def op(self, e, fn, reads=(), writes=(), sig=True):
        waits = self._deps(e, reads, writes)
        if sig:
            self.cnt[e] += 1
            self.pend[e] = False
            tok = (self.sem[e], self.cnt[e])
        else:
            self.pend[e] = True
            tok = (self.sem[e], self.cnt[e] + 1)
        self.prog[e].append((waits, fn, (self.sem[e], 1) if sig else None))
        self._mark(tok, reads, writes)
        return tok

    def dma(self, fn, slot, reads=(), writes=(), e="sp"):
        waits = self._deps(e, reads, writes)
        slot["cnt"] += 16
        tok = (slot["sem"], slot["cnt"])
        self.prog[e].append((waits, fn, (slot["sem"], 16)))
        self._mark(tok, reads, writes)
        return tok

    def dma_group(self, items, slot, e="sp"):
        n = len(items)
        tok = (slot["sem"], slot["cnt"] + 16 * n)
        slot["cnt"] += 16 * n
        for fn, writes in items:
            waits = self._deps(e, (), writes)
            self.prog[e].append((waits, fn, (slot["sem"], 16)))
        for fn, writes in items:
            self._mark(tok, (), writes)
        return tok

    def wait_all(self, e, toks):
        waits = []
        for s, v in toks:
            if self.seen[e].get(s, 0) < v:
                self.seen[e][s] = v
                waits.append((s, v))
        if waits:
            self.prog[e].append((waits, None, None))

    def barrier(self):
        toks = [(self.sem[e], self.cnt[e]) for e in self.ENG if self.cnt[e] > 0]
        toks += [(s["sem"], s["cnt"]) for s in self.slots if s["cnt"] > 0]
        for e in self.ENG:
            assert not self.pend[e], e
            self.wait_all(e, toks)

    def emit(self):
        nc = self.nc
        for e in self.ENG:
            assert not self.pend[e], e
        with nc.Block() as block:
            def run(eng, lst):
                for waits, fn, inc in lst:
                    for s, v in waits:
                        eng.wait_ge(s, v)
                    if fn is not None:
                        ins = fn(eng)
                        if inc is not None:
                            ins.then_inc(inc[0], inc[1])

            @block.tensor
            def _(eng):
                run(eng, self.prog["pe"])

            @block.scalar
            def _(eng):
                run(eng, self.prog["act"])

            @block.vector
            def _(eng):
                run(eng, self.prog["dve"])

            @block.gpsimd
            def _(eng):
                run(eng, self.prog["pool"])

            @block.sync
            def _(eng):
                run(eng, self.prog["sp"])


_DTSZ = {F32: 4, BF16: 2, I32: 4}


class Arena:
    def __init__(self, nc, lo, hi):
        self.nc, self.lo, self.hi, self.ptr = nc, lo, hi, lo
        self.live, self.dead, self.n = [], [], 0
        self.peak = lo

    def alloc(self, name, shape, dt):
        nb = _DTSZ[dt]
        for d in shape[1:]:
            nb *= d
        nb = (nb + 63) // 64 * 64
        off = self.ptr
        self.ptr += nb
        self.peak = max(self.peak, self.ptr)
        assert self.ptr <= self.hi, (name, self.ptr, self.hi)
        self.n += 1
        h = self.nc.alloc_sbuf_tensor_at("%s_%d" % (name, self.n), list(shape), dt, offset=off)
        b = Buf(name)
        for lo2, hi2, b2 in self.dead:
            if lo2 < off + nb and off < hi2:
                if b2.w is not None and b.r.get(b2.w[0], 0) < b2.w[1]:
                    b.r[b2.w[0]] = b2.w[1]
                for s, v in b2.r.items():
                    if b.r.get(s, 0) < v:
                        b.r[s] = v
        self.live.append((off, off + nb, b))
        return T(h, b)

    def mark(self):
        return (self.ptr, len(self.live))

    def release(self, m):
        ptr, n = m
        self.dead.extend(self.live[n:])
        del self.live[n:]
        self.ptr = ptr


def run_pipelined(gens, skew, maxlive=2):
    pending = list(gens)
    active = []
    since = skew
    while pending or active:
        if pending and len(active) < maxlive and since >= skew:
            active.append(pending.pop(0))
            since = 0
        for g in list(active):
            try:
                next(g)
            except StopIteration:
                active.remove(g)
        since += 1


def bc(ap, axis, n):
    a = ap.unsqueeze(axis)
    shp = list(a.shape)
    shp[axis] = n
    return a.broadcast_to(shp)


def build_program(debug=False, limit=None, ng_kv=16, n_m=4):
    nc = bass.Bass("TRN2", target_bir_lowering=False)
    S = Sched(nc)
    D = {}

    def din(name, shape, dt=F32):
        D[name] = nc.dram_tensor(name, list(shape), dt, kind="ExternalInput").ap()

    din("x_kv", [8192, 1024])
    din("x_own", [16, 128, 1024])
    din("x_halo", [32, 1024])
    din("p_own", [16, 128, 256])
    din("pos_kv", [128, 64], I32)
    din("pos_own", [128, 16], I32)
    din("masks", [128, 8, 128])
    din("ident", [128, 128])
    din("inv_freq", [32])
    din("g_in_pp", [128, 8])
    din("g_pl_pp", [128, 8])
    din("g_o_pp", [128, 8])
    din("g_cq_pp", [128, 2])
    din("conv_wT", [128, 4, 3])
    din("w_in", [1024, 3008])
    din("w_uq", [256, 768])
    din("g_ckv", [128])
    din("w_ukv", [128, 1024])
    din("g_q", [192])
    din("g_k", [192])
    din("w_o", [1024, 1024])
    din("w_pl", [256, 1024])
    din("w_plg", [1024, 1024])
    out = nc.dram_tensor("out_own", [16, 128, 1024], F32, kind="ExternalOutput").ap()

    base = (nc.sbuf_base + 63) // 64 * 64
    AR = Arena(nc, base, nc.sbuf_top)
    A = AR.alloc

    bk = [T(nc.alloc_psum_tensor("bk%d" % i, [128, 512], F32), Buf("bk%d" % i, excl=True)) for i in range(8)]

    def bkb(i):
        return bk[i].h[:].bitcast(BF16)

    w_own = A("w_own", [128, 8, 2816], BF16)
    w_kv = A("w_kv", [128, 8, 192], BF16)
    w_uq = A("w_uq", [128, 2, 768], BF16)
    w_uk = A("w_uk", [128, 4, 128], BF16)
    Aabs = A("Aabs", [128, 4, 128], BF16)
    w_uv = A("w_uv", [128, 4, 128], BF16)
    KcT = A("KcT", [128, 8192], BF16)
    KrT = A("KrT", [128, 4096], BF16)
    Vp = A("Vp", [128, 64, 128], BF16)
    rstdk = A("rstdk", [128, 64, 4], F32)
    idf = A("idf", [128, 128], F32)
    idb = A("idb", [128, 128], BF16)
    maskb = A("maskb", [128, 8, 128], BF16)
    invf = A("invf", [128, 32], F32)
    g_in_pp = A("g_in_pp", [128, 8], F32)
    g_pl_pp = A("g_pl_pp", [128, 8], F32)
    g_o_pp = A("g_o_pp", [128, 8], F32)
    g_cq_pp = A("g_cq_pp", [128, 2], F32)
    cw = A("cw", [128, 4, 3], F32)
    gq_pp = A("gq_pp", [128, 1], F32)
    gk_pp = A("gk_pp", [128, 1], F32)
    gqk_pp = A("gqk_pp", [128, 1], F32)
    g_ckv_b = A("g_ckv_b", [128, 128], F32)
    g_kr_b = A("g_kr_b", [128, 64], F32)
    g_qr_b = A("g_qr_b", [128, 64], F32)
    epsb = A("epsb", [128, 1], F32)
    b192 = A("b192", [128, 1], F32)
    zerob = A("zerob", [128, 1], F32)
    ones_bf = A("ones_bf", [128, 128], BF16)
    ones_f = A("ones_f", [128, 1], F32)
    cosQ = A("cosQ", [128, 16, 32], F32)
    sinQ = A("sinQ", [128, 16, 32], F32)
    xThalo = A("xThalo", [128, 8, 16, 2], BF16)
    QpT = A("QpT", [128, 4, 512], BF16)
    QrT = A("QrT", [128, 2, 4, 512], BF16)
    sza = A("sza", [128, 4, 512], BF16)
    ycT = A("ycT", [128, 4, 512], BF16)
    poso_i = A("poso_i", [128, 16], I32)
    m_regionB = AR.mark()
    NXT = 3
    xt = [A("xt%d" % i, [128, 1024], F32) for i in range(NXT)]

    sl_const = S.slot("const")
    sl_const2 = S.slot("const2")
    sl_x = [S.slot("x%d" % i) for i in range(4)]
    for t_ in range(NXT):
        S.dma(lambda e, t_=t_: e.dma_start(out=xt[t_][:], in_=D["x_kv"][t_ * 128:(t_ + 1) * 128, :]), sl_x[t_],
              (), [xt[t_]])
    sl_w = S.slot("wst")
    sl_out = [S.slot("o0"), S.slot("o1")]
    sl_p = [S.slot("p0"), S.slot("p1")]

    def act(out_, in_, func, reads, writes, scale=1.0, bias=None, accum=None, sig=True):
        kw = dict(out=out_, in_=in_, func=func, scale=scale)
        if bias is not None:
            kw["bias"] = bias
        if accum is not None:
            kw["accum_out"] = accum
        return S.op("act", lambda e: e.activation(**kw), reads, writes, sig)

    def tt(eng, out_, in0, in1, op, reads, writes):
        return S.op(eng, lambda e: e.tensor_tensor(out=out_, in0=in0, in1=in1, op=op), reads, writes)

    def ts(eng, out_, in0, s1, s2, op0, op1, reads, writes):
        if s2 is None:
            return S.op(eng, lambda e: e.tensor_scalar(out=out_, in0=in0, scalar1=s1, scalar2=None, op0=op0),
                        reads, writes)
        return S.op(eng, lambda e: e.tensor_scalar(out=out_, in0=in0, scalar1=s1, scalar2=s2, op0=op0, op1=op1),
                    reads, writes)

    def stt(eng, out_, in0, scalar, in1, op0, op1, reads, writes):
        return S.op(eng, lambda e: e.scalar_tensor_tensor(out=out_, in0=in0, scalar=scalar, in1=in1,
                                                          op0=op0, op1=op1), reads, writes)

    def cp(eng, out_, in_, reads, writes):
        if eng == "act":
            return S.op("act", lambda e: e.copy(out=out_, in_=in_), reads, writes)
        return S.op(eng, lambda e: e.tensor_copy(out=out_, in_=in_), reads, writes)

    def sigmoid_act(out_, in_, reads, writes):
        act(out_, in_, AF.Exp, reads + [zerob], writes, scale=-1.0, bias=zerob[:])
        act(out_, out_, AF.Ln, writes + [ones_f], writes, scale=1.0, bias=ones_f[:])
        act(out_, out_, AF.Exp, writes + [zerob], writes, scale=-1.0, bias=zerob[:])

    def recip(out_, in_, reads, writes):
        return S.op("dve", lambda e: e.reciprocal(out=out_, in_=in_), reads, writes)

    def red(out_, in_, reads, writes, eng="dve"):
        return S.op(eng, lambda e: e.tensor_reduce(out=out_, in_=in_, axis=AX.X, op=ALU.add), reads, writes)

    def mm(out_, lhsT, rhs, start, stop, reads, writes, sig=True):
        return S.op("pe", lambda e: e.matmul(out_, lhsT=lhsT, rhs=rhs, start=start, stop=stop), reads, writes, sig)

    def pe_warm(bank, rhs_ap, rhs_t, n):
        for _ in range(n):
            mm(bank.h[:, :], idb[:], rhs_ap, True, True, [idb, rhs_t], [bank], sig=False)

    def tr(out_, in_, ident, reads, writes, sig=True):
        return S.op("pe", lambda e: e.transpose(out=out_, in_=in_, identity=ident), reads, writes, sig)

    def rstd_from(ssq_ap, out_ap, n, reads, writes, bias_t=None):
        P = out_ap.shape[0]
        act(out_ap, ssq_ap, AF.Ln, reads + [epsb], writes, scale=1.0 / n, bias=epsb[0:P])
        bt = zerob if bias_t is None else bias_t
        act(out_ap, out_ap, AF.Exp, writes + [bt], writes, scale=-0.5, bias=bt[0:P])

    cosK = A("cosK", [128, 64, 32], F32)
    sinK = A("sinK", [128, 64, 32], F32)
    wst = A("wst", [128, 1504], F32)
    m_setup = AR.mark()
    masks_f = A("masks_f", [128, 8, 128], F32)
    posk_i = A("posk_i", [128, 64], I32)
    xh_f = A("xh_f", [32, 1024], F32)
    items = [
        (lambda e: e.dma_start(out=idf[:], in_=D["ident"]), [idf]),
        (lambda e: e.dma_start(out=masks_f[:], in_=D["masks"]), [masks_f]),
        (lambda e: e.dma_start(out=invf[:], in_=D["inv_freq"].partition_broadcast(128)), [invf]),
        (lambda e: e.dma_start(out=posk_i[:], in_=D["pos_kv"]), [posk_i]),
        (lambda e: e.dma_start(out=poso_i[:], in_=D["pos_own"]), [poso_i]),
        (lambda e: e.dma_start(out=g_in_pp[:], in_=D["g_in_pp"]), [g_in_pp]),
        (lambda e: e.dma_start(out=g_pl_pp[:], in_=D["g_pl_pp"]), [g_pl_pp]),
        (lambda e: e.dma_start(out=g_o_pp[:], in_=D["g_o_pp"]), [g_o_pp]),
        (lambda e: e.dma_start(out=g_cq_pp[:], in_=D["g_cq_pp"]), [g_cq_pp]),
        (lambda e: e.dma_start(out=cw[:], in_=D["conv_wT"]), [cw]),
        (lambda e: e.dma_start(out=gq_pp[:], in_=D["g_q"][0:128].rearrange("(p o) -> p o", o=1)), [gq_pp]),
        (lambda e: e.dma_start(out=gk_pp[:], in_=D["g_k"][0:128].rearrange("(p o) -> p o", o=1)), [gk_pp]),
        (lambda e: e.dma_start(out=g_ckv_b[:], in_=D["g_ckv"].partition_broadcast(128)), [g_ckv_b]),
        (lambda e: e.dma_start(out=g_kr_b[:], in_=D["g_k"][128:192].partition_broadcast(128)), [g_kr_b]),
        (lambda e: e.dma_start(out=g_qr_b[:], in_=D["g_q"][128:192].partition_broadcast(128)), [g_qr_b]),
        (lambda e: e.dma_start(out=xh_f[:], in_=D["x_halo"]), [xh_f]),
    ]
    late = [it_ for it_ in items if it_[1][0] is masks_f or it_[1][0] is xh_f]
    early = [it_ for it_ in items if not (it_[1][0] is masks_f or it_[1][0] is xh_f)]
    S.dma_group(early, sl_const)
    S.dma_group(late, sl_const2)
    S.op("pool", lambda e: e.memset(epsb[:], EPS), (), [epsb])
    S.op("pool", lambda e: e.memset(b192[:], -0.5 * math.log(192.0)), (), [b192])
    S.op("pool", lambda e: e.memset(zerob[:], 0.0), (), [zerob])
    S.op("pool", lambda e: e.memset(ones_bf[:], 1.0), (), [ones_bf])
    S.op("pool", lambda e: e.memset(ones_f[:], 1.0), (), [ones_f])
    S.op("pool", lambda e: e.memset(QrT[:], 0.0), (), [QrT])
    cp("pool", idb[:], idf[:], [idf], [idb])
    cp("pool", maskb[:], masks_f[:], [masks_f], [maskb])
    tt("dve", gqk_pp[:], gq_pp[:], gk_pp[:], ALU.mult, [gq_pp, gk_pp], [gqk_pp])

    TWO_PI = 2.0 * math.pi
    SIN_SCALE = 6.283185

    def rope_tables(pos_i, ntile, cos_t, sin_t):
        mk = AR.mark()
        n = ntile * 32
        posf = A("posf", [128, ntile], F32)
        ang = A("ang", [128, ntile, 32], F32)
        cp("dve", posf[:], pos_i[:], [pos_i], [posf])
        tt("dve", ang[:], bc(posf[:], 2, 32), bc(invf[:], 1, ntile), ALU.mult, [posf, invf], [ang])
        angf = ang[:].rearrange("p a b -> p (a b)")
        u = A("u", [128, n], F32)
        nf = A("nf", [128, n], F32)
        for tab, off, eng in ((sin_t, 0.0, "dve"), (cos_t, 0.25, "dve")):
            ni = A("ni", [128, n], I32)
            ts(eng, u[:], angf, 1.0 / TWO_PI, off, ALU.mult, ALU.add, [ang], [u])
            cp(eng, ni[:], u[:], [u], [ni])
            cp(eng, nf[:], ni[:], [ni], [nf])
            tt(eng, u[:], u[:], nf[:], ALU.subtract, [u, nf], [u])
            if eng == "dve":
                stt(eng, nf[:], u[:], 0.5, u[:], ALU.is_gt, ALU.subtract, [u], [nf])
                stt(eng, u[:], nf[:], 0.5, nf[:], ALU.is_gt, ALU.subtract, [nf], [u])
            else:
                ts(eng, nf[:], u[:], 0.5, None, ALU.is_gt, None, [u], [nf])
                tt(eng, u[:], u[:], nf[:], ALU.subtract, [u, nf], [u])
                ts(eng, nf[:], u[:], -0.5, None, ALU.is_lt, None, [u], [nf])
                tt(eng, u[:], u[:], nf[:], ALU.add, [u, nf], [u])
            act(tab[:].rearrange("p a b -> p (a b)"), u[:], AF.Sin, [u, zerob], [tab], scale=SIN_SCALE, bias=zerob[:])
        AR.release(mk)


    def finish():
        S.barrier()
        S.emit()
        return nc, AR.peak
    if limit == "tables":
        return finish()

    rope_tables(posk_i, 64, cosK, sinK)

    mk = AR.mark()
    hj = A("hj", [32, 1024], BF16)
    hs = A("hs", [32, 1], F32)
    hb = A("hb", [32, 1024], BF16)
    act(hj[:], xh_f[:], AF.Square, [xh_f], [hj, hs], accum=hs[:])
    rstd_from(hs[:], hs[:], 1024, [hs], [hs])
    ts("dve", hb[:], xh_f[:], hs[:], None, ALU.mult, None, [xh_f, hs], [hb])
    pth = bkb(0).rearrange("p (c t) -> p c t", c=8)
    for c in range(8):
        tr(pth[:, c, 0:32], hb[:, c * 128:(c + 1) * 128], idb[0:32, 0:32], [hb, idb], [bk[0]], sig=(c == 7))
    cp("dve", xThalo[:].rearrange("p c t h -> p c (t h)"), pth[:, :, 0:32], [bk[0]], [xThalo])
    AR.release(mk)
    AR.release(m_setup)

    for hf in range(2):
        src = D["w_in"][hf * 512:(hf + 1) * 512, 256:448].rearrange("(c p) n -> p c n", p=128)
        dstv = wst[:, 0:768].rearrange("p (c n) -> p c n", c=4)
        S.dma(lambda e, src=src, dstv=dstv: e.dma_start(out=dstv, in_=src), sl_w, (), [wst])
        tt("dve", w_kv[:, hf * 4:(hf + 1) * 4, :], dstv, bc(g_in_pp[:, hf * 4:(hf + 1) * 4], 2, 192), ALU.mult,
           [wst, g_in_pp], [w_kv])
    S.dma(lambda e: e.dma_start(out=wst[:, 0:1024], in_=D["w_ukv"]), sl_w, (), [wst])
    wukv = wst[:, 0:1024].rearrange("p (h n) -> p h n", h=4)
    cp("dve", w_uk[:], wukv[:, :, 0:128], [wst], [w_uk])
    cp("dve", w_uv[:], wukv[:, :, 128:256], [wst], [w_uv])
    for h in range(4):
        S.op("pe", lambda e, h=h: e.transpose(out=bk[1].h[:, h * 128:(h + 1) * 128], in_=wukv[:, h, 0:128],
                                             identity=idf[:]), [wst, idf], [bk[1]], sig=(h == 3))
    ts("dve", Aabs[:].rearrange("p h n -> p (h n)"), bk[1].h[:, :], gqk_pp[:], None, ALU.mult, None,
       [bk[1], gqk_pp], [Aabs])

    if limit == "setup":
        return finish()
    wsts = [wst, A("wst_b", [128, 1504], F32)]
    sl_ws = [sl_w, S.slot("wst_b")]
    job_list = []
    wdst_buf = Buf("wdst")
    gain_src = Buf("gains")
    gain_src.w = g_in_pp.b.w
    for c in range(8):
        rows = D["w_in"][c * 128:(c + 1) * 128, :]
        job_list.append((rows[:, 0:1504], 1504,
                         [(w_own[:, c, 0:256], 0, 256), (w_own[:, c, 256:1312], 448, 1504)], g_in_pp[:, c:c + 1]))
        job_list.append((rows[:, 1504:3008], 1504, [(w_own[:, c, 1312:2816], 0, 1504)], g_in_pp[:, c:c + 1]))
    for c in range(2):
        job_list.append((D["w_uq"][c * 128:(c + 1) * 128, :], 768, [(w_uq[:, c, :], 0, 768)], g_cq_pp[:, c:c + 1]))
    job_state = dict(issued=0, done=0)

    def job_issue():
        k = job_state["issued"]
        if k >= len(job_list):
            return
        src_ap, ncols, pieces, gain_ap = job_list[k]
        w_ = wsts[k % 2]
        S.dma(lambda e: e.dma_start(out=w_[:, 0:ncols], in_=src_ap), sl_ws[k % 2], (), [w_])
        job_state["issued"] = k + 1

    def job_consume():
        k = job_state["done"]
        if k >= len(job_list):
            return
        src_ap, ncols, pieces, gain_ap = job_list[k]
        w_ = wsts[k % 2]
        for dst, lo, hi in pieces:
            ts("dve", dst, w_[:, lo:hi], gain_ap, None, ALU.mult, None, [w_, gain_src], [wdst_buf])
        job_state["done"] = k + 1
        job_issue()

    xb = [A("xb0", [128, 1024], BF16), A("xb1", [128, 1024], BF16)]
    sqj1 = A("sqj", [128, 1024], BF16)
    sqjs = [sqj1, sqj1]
    xTg = [A("xTg0", [128, 8, 512], BF16), A("xTg1", [128, 8, 512], BF16)]
    ssqx = [A("ssqx0", [128, 4], F32), A("ssqx1", [128, 4], F32)]
    rstdx = [A("rstdx0", [128, 4], F32), A("rstdx1", [128, 4], F32)]
    csts = [A("cst0", [128, 4, 192], F32), A("cst1", [128, 4, 192], F32)]
    scr = A("scr", [128, 768], F32)
    sqk = A("sqk", [128, 4, 512], BF16)
    ssqc = A("ssqc", [128, 4], F32)
    ssqps = [A("ssqp0", [128, 4], F32), A("ssqp1", [128, 4], F32)]
    rstdc = A("rstdc", [128, 4], F32)
    vtmp = A("vtmp", [128, 4, 128], F32)
    ssqkn = A("ssqkn", [128, 4, 4], F32)
    krg = A("krg", [128, 4, 64], F32)
    ra = A("ra", [128, 4, 64], F32)
    rb = A("rb", [128, 4, 64], F32)
    krd = A("krd", [128, 4, 128], BF16)
    NG_KV = ng_kv
    NT = 4 * NG_KV

    def ld_x(t):
        S.dma(lambda e: e.dma_start(out=xt[t % NXT][:], in_=D["x_kv"][t * 128:(t + 1) * 128, :]), sl_x[t % NXT],
              (), [xt[t % NXT]])

    def f_sc(t):
        G, tq, tp = t // 4, t % 4, t % 2
        gp = G % 2
        sqj = sqjs[tp]
        xt_ = xt[t % NXT]
        act(sqj[:], xt_[:], AF.Square, [xt_], [sqj, ssqx[gp]], accum=ssqx[gp][:, tq:tq + 1])
        cp("dve", xb[tp][:], xt_[:], [xt_], [xb[tp]])
        if t + NXT < NT:
            ld_x(t + NXT)

    def f_te(t):
        G, tq, tp = t // 4, t % 4, t % 2
        gp = G % 2
        ptr = bkb(tp).rearrange("p (c t) -> p c t", c=8)
        for c in range(8):
            tr(ptr[:, c, :], xb[tp][:, c * 128:(c + 1) * 128], idb[:], [xb[tp], idb], [bk[tp]], sig=(c == 7))
        cp("dve", xTg[gp][:, :, tq * 128:(tq + 1) * 128], ptr, [bk[tp]], [xTg[gp]])

    def kv_front(G):
        for tq in range(4):
            t = 4 * G + tq
            if t + 1 < NT:
                f_sc(t + 1)
            yield
            f_te(t)
            yield

    def kv_back_a(G):
        gp = G % 2
        cst = csts[gp]
        ssqp = ssqps[gp]
        for tq in range(4):
            pcv = bk[2 + tq // 2].h[:, (tq % 2) * 192:(tq % 2) * 192 + 192]
            for c in range(8):
                mm(pcv, xTg[gp][:, c, tq * 128:(tq + 1) * 128], w_kv[:, c, :], c == 0, c == 7,
                   [xTg[gp], w_kv], [bk[2 + tq // 2]], sig=(c == 7))
            if tq % 2:
                yield
        rstd_from(ssqx[gp][:], rstdx[gp][:], 1024, [ssqx[gp]], [rstdx[gp]])
        yield
        for tq in range(4):
            pcv = bk[2 + tq // 2].h[:, (tq % 2) * 192:(tq % 2) * 192 + 192]
            act(cst[:, tq, :], pcv, AF.Copy, [bk[2 + tq // 2], rstdx[gp]], [cst], scale=rstdx[gp][:, tq:tq + 1])
            if tq % 2:
                yield
        cstf = cst[:].rearrange("p a b -> p (a b)")
        tt("dve", scr[:], cstf, cstf, ALU.mult, [cst], [scr])
        yield
        scr3 = scr[:].rearrange("p (a b) -> p a b", a=4)
        red(ssqc[:], scr3[:, :, 0:128], [scr], [ssqc])
        red(ssqp[:], scr3[:, :, 128:192], [scr], [ssqp])
        yield
        rstd_from(ssqc[:], rstdc[:], 128, [ssqc], [rstdc])
        yield
        tt("dve", vtmp[:], cst[:, :, 0:128], bc(rstdc[:], 2, 128), ALU.mult, [cst, rstdc], [vtmp])
        yield
        tt("dve", Vp[:, 4 * G:4 * G + 4, :], vtmp[:], bc(g_ckv_b[:], 1, 4), ALU.mult, [vtmp, g_ckv_b], [Vp])
        yield
        ptv = bkb(6).rearrange("p (c t) -> p c t", c=8)
        for tq in range(4):
            tr(ptv[:, tq, :], Vp[:, 4 * G + tq, :], idb[:], [Vp, idb], [bk[6]], sig=(tq == 3))
        yield
        cp("act", KcT[:, G * 512:(G + 1) * 512].rearrange("p (a b) -> p a b", a=4), ptv[:, 0:4, :], [bk[6]], [KcT])
        job_consume()

    def kv_back_b(G):
        gp = G % 2
        cst = csts[gp]
        ssqp = ssqps[gp]
        tt("pool", krg[:], cst[:, :, 128:192], bc(g_kr_b[:], 1, 4), ALU.mult, [cst, g_kr_b], [krg])
        cK = cosK[:, 4 * G:4 * G + 4, :]
        sK = sinK[:, 4 * G:4 * G + 4, :]
        tt("pool", ra[:, :, 0:32], krg[:, :, 0:32], cK, ALU.mult, [krg, cosK], [ra])
        tt("pool", ra[:, :, 32:64], krg[:, :, 32:64], cK, ALU.mult, [krg, cosK], [ra])
        yield
        for tq in range(4):
            pb = bk[4 + tq % 2]
            mm(pb.h[:, :], KcT[:, (4 * G + tq) * 128:(4 * G + tq + 1) * 128], w_uk[:].rearrange("p h n -> p (h n)"),
               True, True, [KcT, w_uk], [pb])
            if tq == 0:
                tt("pool", rb[:, :, 0:32], krg[:, :, 32:64], sK, ALU.mult, [krg, sinK], [rb])
                tt("pool", rb[:, :, 32:64], krg[:, :, 0:32], sK, ALU.mult, [krg, sinK], [rb])
            if tq == 1:
                tt("pool", krd[:, :, 0:32], ra[:, :, 0:32], rb[:, :, 0:32], ALU.subtract, [ra, rb], [krd])
                tt("pool", krd[:, :, 32:64], ra[:, :, 32:64], rb[:, :, 32:64], ALU.add, [ra, rb], [krd])
            if tq == 2:
                cp("pool", krd[:, :, 64:128], krd[:, :, 0:64], [krd], [krd])
            yield
            act(sqk[:, tq, :], pb.h[:, :], AF.Square, [pb], [sqk])
            yield
        red(ssqkn[:].rearrange("p a b -> p (a b)"), sqk[:].rearrange("p a (h n) -> p (a h) n", h=4), [sqk], [ssqkn])
        yield
        tt("dve", ssqkn[:], ssqkn[:], bc(ssqp[:], 2, 4), ALU.add, [ssqkn, ssqp], [ssqkn])
        yield
        rk = rstdk[:, 4 * G:4 * G + 4, :]
        act(rk, ssqkn[:], AF.Ln, [ssqkn, epsb], [rstdk], scale=1.0 / 192, bias=epsb[:])
        act(rk, rk, AF.Exp, [rstdk, b192], [rstdk], scale=-0.5, bias=b192[:])
        yield
        ptk = bkb(7).rearrange("p (c t) -> p c t", c=8)
        for tq in range(4):
            tr(ptk[:, tq, :], krd[:, tq, :], idb[:], [krd, idb], [bk[7]], sig=(tq == 3))
        yield
        hp = 0 if G < 8 else 64
        col0 = (G % 8) * 512
        cp("act", KrT[hp:hp + 64, col0:col0 + 512].rearrange("p (a b) -> p a b", a=4), ptk[hp:hp + 64, 0:4, :],
           [bk[7]], [KrT])
        job_consume()

    job_issue()
    job_issue()
    f_sc(0)
    for r in range(NG_KV + 2):
        gens = []
        if r < NG_KV:
            gens.append(kv_front(r))
        if 0 <= r - 1 < NG_KV:
            gens.append(kv_back_a(r - 1))
        if 0 <= r - 2 < NG_KV:
            gens.append(kv_back_b(r - 2))
        run_pipelined(gens, skew=0, maxlive=3)
    while job_state["done"] < len(job_list):
        job_consume()

    S.barrier()
    if limit == "kv":
        S.emit()
        return nc, AR.peak
    AR.release(m_regionB)

    w_o = A("w_o", [128, 8, 1024], BF16)
    w_plg = A("w_plg", [128, 8, 1024], BF16)
    w_pl = A("w_pl", [128, 2, 1024], BF16)
    wepi_jobs = []
    for (src, dstT, nchunk, gain) in ((D["w_o"], w_o, 8, g_o_pp), (D["w_plg"], w_plg, 8, g_pl_pp),
                                      (D["w_pl"], w_pl, 2, None)):
        for c in range(nchunk):
            wepi_jobs.append((src, dstT, c, gain))
    wepi_state = dict(issued=0, done=0, bufs=None, slots=[S.slot("w2_%d" % i) for i in range(4)])

    def wepi_issue():
        k = wepi_state["issued"]
        if k >= len(wepi_jobs):
            return
        src, dstT, c, gain = wepi_jobs[k]
        ws = wepi_state["bufs"][k % 4]
        S.dma(lambda e: e.dma_start(out=ws[:], in_=src[c * 128:(c + 1) * 128, :]), wepi_state["slots"][k % 4], (), [ws])
        wepi_state["issued"] = k + 1

    def wepi_consume():
        k = wepi_state["done"]
        if k >= len(wepi_jobs):
            return
        src, dstT, c, gain = wepi_jobs[k]
        ws = wepi_state["bufs"][k % 4]
        if gain is None:
            cp("dve", dstT[:, c, :], ws[:], [ws], [dstT])
        elif k % 2:
            act(dstT[:, c, :], ws[:], AF.Copy, [ws, gain], [dstT], scale=gain[:, c:c + 1])
        else:
            ts("dve", dstT[:, c, :], ws[:], gain[:, c:c + 1], None, ALU.mult, None, [ws, gain], [dstT])
        wepi_state["done"] = k + 1
        wepi_issue()

    if limit == "wepi":
        return finish()

    sl_xo = [S.slot("xo0"), S.slot("xo1")]
    sl_xr = [S.slot("xr0"), S.slot("xr1")]
    outst = [A("outst0", [128, 1024], F32), A("outst1", [128, 1024], F32)]
    out_toks = []

    for m in range(n_m):
        mA = AR.mark()
        xTh = A("xTh", [128, 8, 4, 130], BF16)

        mL1 = AR.mark()
        xos = [A("xo0", [128, 1024], F32), A("xo1", [128, 1024], F32)]
        xobs = [A("xob0", [128, 1024], BF16), A("xob1", [128, 1024], BF16)]
        sjs = [A("sj0", [128, 1024], BF16), A("sj1", [128, 1024], BF16)]
        sos = [A("so0", [128, 1], F32), A("so1", [128, 1], F32)]

        def l1_tile(s):
            t = 4 * m + s
            xo, xob, sj, so = xos[s % 2], xobs[s % 2], sjs[s % 2], sos[s % 2]
            S.dma(lambda e: e.dma_start(out=xo[:], in_=D["x_own"][t]), sl_xo[s % 2], (), [xo])
            yield
            act(sj[:], xo[:], AF.Square, [xo], [sj, so], accum=so[:])
            yield
            rstd_from(so[:], so[:], 1024, [so], [so])
            yield
            ts("dve", xob[:], xo[:], so[:], None, ALU.mult, None, [xo, so], [xob])
            yield
            ptr = bkb(s % 2).rearrange("p (c t) -> p c t", c=8)
            for c in range(8):
                tr(ptr[:, c, :], xob[:, c * 128:(c + 1) * 128], idb[:], [xob, idb], [bk[s % 2]], sig=(c == 7))
            yield
            cp("dve", xTh[:, :, s, 2:130], ptr, [bk[s % 2]], [xTh])

        run_pipelined([l1_tile(s) for s in range(4)], skew=1, maxlive=2)
        cp("pool", xTh[:, :, :, 0:2], xThalo[:, :, 4 * m:4 * m + 4, :], [xThalo], [xTh])
        AR.release(mL1)
        if m == 0:
            rope_tables(poso_i, 16, cosQ, sinQ)
        if limit == "A1a":
            return finish()

        mQ = AR.mark()
        cqnT = A("cqnT", [128, 2, 512], BF16)
        qnT = A("qnT", [128, 4, 512], BF16)
        mQ1 = AR.mark()
        cq_sb = A("cq_sb", [128, 4, 256], F32)
        scq = A("scq", [128, 1024], F32)
        s4 = A("s4", [128, 4], F32)
        r4 = A("r4", [128, 4], F32)
        cqn = A("cqn", [128, 4, 256], BF16)
        ezs = [A("ez0", [128, 512], F32), A("ez1", [128, 512], F32)]

        def cq_chain():
            for s in range(4):
                pv = bk[2 + s // 2].h[:, (s % 2) * 256:(s % 2) * 256 + 256]
                for c in range(8):
                    mm(pv, xTh[:, c, s, 2:130], w_own[:, c, 0:256], c == 0, c == 7, [xTh, wdst_buf],
                       [bk[2 + s // 2]], sig=(c == 7))
                yield
            for b2 in range(2):
                cp("act", cq_sb[:, 2 * b2:2 * b2 + 2, :].rearrange("p a b -> p (a b)"), bk[2 + b2].h[:, :],
                   [bk[2 + b2]], [cq_sb])
            yield
            cqf = cq_sb[:].rearrange("p a b -> p (a b)")
            tt("dve", scq[:], cqf, cqf, ALU.mult, [cq_sb], [scq])
            red(s4[:], scq[:].rearrange("p (a b) -> p a b", a=4), [scq], [s4])
            yield
            rstd_from(s4[:], r4[:], 256, [s4], [r4])
            yield
            tt("dve", cqn[:], cq_sb[:], bc(r4[:], 2, 256), ALU.mult, [cq_sb, r4], [cqn])
            yield
            ptq = bkb(4).rearrange("p (c s t) -> p c s t", c=2, s=4)
            for s in range(4):
                for c2 in range(2):
                    tr(ptq[:, c2, s, :], cqn[:, s, c2 * 128:(c2 + 1) * 128], idb[:], [cqn, idb], [bk[4]],
                       sig=(s == 3 and c2 == 1))
            yield
            cp("dve", cqnT[:].rearrange("p c t -> p (c t)"), bkb(4), [bk[4]], [cqnT])

        def za_tile(s):
            pz = bk[5 + s % 2]
            ez = ezs[s % 2]
            for c in range(8):
                mm(pz.h[:, :], xTh[:, c, s, 2:130], w_own[:, c, 256:768], c == 0, c == 7, [xTh, wdst_buf], [pz],
                   sig=(c == 7))
            yield
            sigmoid_act(ez[:], pz.h[:, :], [pz], [ez])
            yield
            tt("dve", sza[:, s, :], pz.h[:, :], ez[:], ALU.mult, [pz, ez], [sza])

        run_pipelined([cq_chain()] + [za_tile(s) for s in range(4)], skew=2, maxlive=3)
        AR.release(mQ1)
        if limit == "A1b":
            return finish()

        q_sbs = [A("q_sb0", [128, 4, 192], F32), A("q_sb1", [128, 4, 192], F32)]
        sq3 = A("sq3", [128, 768], F32)
        s4s = [A("s4a", [128, 4], F32), A("s4b", [128, 4], F32)]
        r4s = [A("r4a", [128, 4], F32), A("r4b", [128, 4], F32)]
        qnbs = [A("qnb0", [128, 4, 128], BF16), A("qnb1", [128, 4, 128], BF16)]
        qrs = [A("qr0", [128, 4, 64], F32), A("qr1", [128, 4, 64], F32)]
        qa = A("qa", [128, 4, 64], F32)
        qb = A("qb", [128, 4, 64], F32)
        qrbs = [A("qrb0", [128, 4, 128], BF16), A("qrb1", [128, 4, 128], BF16)]

        def q_tile(s):
            t = 4 * m + s
            par = s % 2
            pa, pb = bk[2 * par], bk[2 * par + 1]
            q_sb, s4_, r4_, qnb, qr, qrb = q_sbs[par], s4s[par], r4s[par], qnbs[par], qrs[par], qrbs[par]
            for c2 in range(2):
                mm(pa.h[:, :], cqnT[:, c2, s * 128:(s + 1) * 128], w_uq[:, c2, 0:512], c2 == 0, c2 == 1,
                   [cqnT, wdst_buf], [pa], sig=False)
            for c2 in range(2):
                mm(pb.h[:, 0:256], cqnT[:, c2, s * 128:(s + 1) * 128], w_uq[:, c2, 512:768], c2 == 0, c2 == 1,
                   [cqnT, wdst_buf], [pb], sig=(c2 == 1))
            yield
            qf = q_sb[:].rearrange("p a b -> p (a b)")
            cp("act", qf[:, 0:512], pa.h[:, :], [pa], [q_sb])
            cp("act", qf[:, 512:768], pb.h[:, 0:256], [pb], [q_sb])
            yield
            tt("dve", sq3[:], qf, qf, ALU.mult, [q_sb], [sq3])
            red(s4_[:], sq3[:].rearrange("p (a b) -> p a b", a=4), [sq3], [s4_])
            yield
            rstd_from(s4_[:], r4_[:], 192, [s4_], [r4_])
            yield
            tt("dve", qnb[:], q_sb[:, :, 0:128], bc(r4_[:], 2, 128), ALU.mult, [q_sb, r4_], [qnb])
            tt("dve", qr[:], q_sb[:, :, 128:192], bc(r4_[:], 2, 64), ALU.mult, [q_sb, r4_], [qr])
            yield
            tt("dve", qr[:], qr[:], bc(g_qr_b[:], 1, 4), ALU.mult, [qr, g_qr_b], [qr])
            cQ = bc(cosQ[:, t, :], 1, 4)
            sQ = bc(sinQ[:, t, :], 1, 4)
            tt("dve", qa[:, :, 0:32], qr[:, :, 0:32], cQ, ALU.mult, [qr, cosQ], [qa])
            tt("dve", qa[:, :, 32:64], qr[:, :, 32:64], cQ, ALU.mult, [qr, cosQ], [qa])
            yield
            tt("dve", qb[:, :, 0:32], qr[:, :, 32:64], sQ, ALU.mult, [qr, sinQ], [qb])
            tt("dve", qb[:, :, 32:64], qr[:, :, 0:32], sQ, ALU.mult, [qr, sinQ], [qb])
            yield
            tt("dve", qrb[:, :, 0:32], qa[:, :, 0:32], qb[:, :, 0:32], ALU.subtract, [qa, qb], [qrb])
            tt("dve", qrb[:, :, 32:64], qa[:, :, 32:64], qb[:, :, 32:64], ALU.add, [qa, qb], [qrb])
            cp("dve", qrb[:, :, 64:128], qrb[:, :, 0:64], [qrb], [qrb])
            yield
            ptn = bkb(4 + par).rearrange("p (c t) -> p c t", c=8)
            for h in range(4):
                tr(ptn[:, h, :], qnb[:, h, :], idb[:], [qnb, idb], [bk[4 + par]], sig=False)
            for h in range(4):
                tr(ptn[:, 4 + h, :], qrb[:, h, :], idb[:], [qrb, idb], [bk[4 + par]], sig=(h == 3))
            yield
            cp("dve", qnT[:, :, s * 128:(s + 1) * 128], ptn[:, 0:4, :], [bk[4 + par]], [qnT])
            cp("act", QrT[0:64, 0, :, s * 128:(s + 1) * 128], ptn[0:64, 4:8, :], [bk[4 + par]], [QrT])
            cp("act", QrT[64:128, 1, :, s * 128:(s + 1) * 128], ptn[64:128, 4:8, :], [bk[4 + par]], [QrT])

        run_pipelined([q_tile(s) for s in range(4)], skew=4, maxlive=2)
        for h in range(4):
            pb_ = bk[6 + h % 2]
            mm(pb_.h[:, :], Aabs[:, h, :], qnT[:, h, :], True, True, [Aabs, qnT], [pb_])
            cp("act" if h % 2 else "dve", QpT[:, h, :], pb_.h[:, :], [pb_], [QpT])
        AR.release(mQ)
        if limit == "A1":
            return finish()

        ccss = [A("ccs0", [128, 2, 130], F32), A("ccs1", [128, 2, 130], F32)]
        prods = [A("prod0", [128, 4, 130], F32), A("prod1", [128, 4, 130], F32)]
        uus = [A("uu0", [128, 4, 128], F32), A("uu1", [128, 4, 128], F32)]
        e2s = [A("e20", [128, 512], F32), A("e21", [128, 512], F32)]
        g1s = [A("g10", [128, 512], F32), A("g11", [128, 512], F32)]
        gT = A("gT", [128, 4, 512], F32)
        sqb4 = A("sqb4", [128, 4, 512], BF16)
        rbc = A("rbc", [128, 512], F32)
        CB, CC, CX, ZC = 768, 1280, 1792, 2304

        def conv_chunk(j):
            par = j % 2
            ccs, prod, uu, e2, g1 = ccss[par], prods[par], uus[par], e2s[par], g1s[par]
            pcc, pcx, pcb, pzc = bk[4 * par], bk[4 * par + 1], bk[4 * par + 2], bk[4 * par + 3]
            for tp in range(2):
                for c in range(8):
                    mm(pcc.h[:, 0:260].rearrange("p (a b) -> p a b", a=2), w_own[:, c, CC + j * 128:CC + (j + 1) * 128],
                       xTh[:, c, 2 * tp:2 * tp + 2, :], c == 0, c == 7, [xTh, wdst_buf], [pcc], sig=(c == 7))
                for c in range(8):
                    mm(pcx.h[:, 0:260].rearrange("p (a b) -> p a b", a=2), w_own[:, c, CX + j * 128:CX + (j + 1) * 128],
                       xTh[:, c, 2 * tp:2 * tp + 2, :], c == 0, c == 7, [xTh, wdst_buf], [pcx], sig=(c == 7))
                yield
                cp("act", ccs[:].rearrange("p a b -> p (a b)"), pcc.h[:, 0:260], [pcc], [ccs])
                yield
                tt("dve", prod[:, 2 * tp:2 * tp + 2, :].rearrange("p a b -> p (a b)"),
                   ccs[:].rearrange("p a b -> p (a b)"), pcx.h[:, 0:260], ALU.mult, [ccs, pcx], [prod])
                yield
            for c in range(8):
                mm(pcb.h[:, :].rearrange("p (a b) -> p a b", a=4), w_own[:, c, CB + j * 128:CB + (j + 1) * 128],
                   xTh[:, c, :, 2:130], c == 0, c == 7, [xTh, wdst_buf], [pcb], sig=(c == 7))
            for c in range(8):
                mm(pzc.h[:, :].rearrange("p (a b) -> p a b", a=4), w_own[:, c, ZC + j * 128:ZC + (j + 1) * 128],
                   xTh[:, c, :, 2:130], c == 0, c == 7, [xTh, wdst_buf], [pzc], sig=(c == 7))
            ts("dve", uu[:], prod[:, :, 0:128], cw[:, j, 0:1], None, ALU.mult, None, [prod, cw], [uu])
            yield
            stt("dve", uu[:], prod[:, :, 1:129], cw[:, j, 1:2], uu[:], ALU.mult, ALU.add, [prod, cw, uu], [uu])
            sigmoid_act(e2[:], pzc.h[:, :], [pzc], [e2])
            yield
            stt("dve", uu[:], prod[:, :, 2:130], cw[:, j, 2:3], uu[:], ALU.mult, ALU.add, [prod, cw, uu], [uu])
            yield
            tt("dve", e2[:], pzc.h[:, :], e2[:], ALU.mult, [pzc, e2], [e2])
            tt("dve", g1[:], pcb.h[:, :], uu[:].rearrange("p a b -> p (a b)"), ALU.mult, [pcb, uu], [g1])
            yield
            tt("dve", gT[:, j, :], g1[:], e2[:], ALU.mult, [g1, e2], [gT])
            yield
            act(sqb4[:, j, :], gT[:, j, :], AF.Square, [gT], [sqb4])

        run_pipelined([conv_chunk(j) for j in range(4)], skew=5, maxlive=2)
        for j in range(4):
            mm(bk[0].h[:, :], ones_bf[:], sqb4[:, j, :], j == 0, j == 3, [ones_bf, sqb4], [bk[0]], sig=(j == 3))
        act(rbc[:], bk[0].h[:, :], AF.Ln, [bk[0], epsb], [rbc], scale=1.0 / 512, bias=epsb[:])
        act(rbc[:], rbc[:], AF.Exp, [rbc, zerob], [rbc], scale=-0.5, bias=zerob[:])
        for j in range(4):
            tt("dve", ycT[:, j, :], gT[:, j, :], rbc[:], ALU.mult, [gT, rbc], [ycT])
        AR.release(mA)
        if limit == "A2":
            return finish()

        mAt = AR.mark()
        NPT = 6
        OTb = A("OTb", [128, 4, 512], BF16)
        rsum = A("rsum", [128, 4, 4], F32)
        mPT = AR.mark()
        pt = [A("pt%d" % i, [128, 512], BF16) for i in range(NPT)]
        racc = [A("racc0", [128, 512], F32), A("racc1", [128, 512], F32)]
        if m == 0:
            wepi_state["bufs"] = [A("wst2_%d" % i, [128, 1024], F32) for i in range(4)]
            for _ in range(4):
                wepi_issue()
        nk = 16 * m + 16
        it = 0
        def finish_head(hh):
            cp("dve", OTb[:, hh, :], bk[4 + hh % 2].h[:, :], [bk[4 + hh % 2]], [OTb])
            for s_ in range(4):
                mm(bk[6].h[:, s_ * 4 + hh:s_ * 4 + hh + 1], racc[hh % 2][:, s_ * 128:(s_ + 1) * 128], ones_f[:],
                   True, True, [racc[hh % 2], ones_f], [bk[6]], sig=(s_ == 3))

        for h in range(4):
            hp = (h % 2) * 64
            po = bk[4 + h % 2]
            rc = racc[h % 2]

            def s_mm(kb, it_):
                r = kb - 16 * m
                s0 = 0 if r < 0 else r // 4
                c0 = s0 * 128
                ps = bk[it_ % 4]
                kp = 0 if kb < 32 else 64
                kc = (kb % 32) * 128
                mm(ps.h[:, c0:512], KcT[:, kb * 128:(kb + 1) * 128], QpT[:, h, c0:512], True, False, [QpT], [ps],
                   sig=False)
                mm(ps.h[:, c0:512], KrT[:, kc:kc + 128], QrT[:, kp // 64, h, c0:512], False, r < 0,
                   [QrT], [ps], sig=(r < 0))
                if r >= 0:
                    midx = (s0 % 2) * 4 + (r % 4)
                    mm(ps.h[:, c0:c0 + 128], idb[:], maskb[:, midx, :], False, True, [idb, maskb], [ps])
                return c0

            def c0_of(kb):
                r = kb - 16 * m
                return 0 if r < 0 else (r // 4) * 128

            s_mm(0, it)
            if nk > 1:
                s_mm(1, it + 1)
            for kb in range(nk):
                c0 = c0_of(kb)
                ps = bk[it % 4]
                if kb + 2 < nk:
                    s_mm(kb + 2, it + 2)
                ptb = pt[it % NPT]
                act(ptb[:, c0:512], ps.h[:, c0:512], AF.Exp, [ps, zerob], [ptb], scale=rstdk[:, kb, h:h + 1],
                    bias=zerob[:])
                if kb == 0:
                    cp("dve", rc[:], ptb[:], [ptb], [rc])
                else:
                    tt("dve", rc[:, c0:512], rc[:, c0:512], ptb[:, c0:512], ALU.add, [rc, ptb], [rc])
                mm(po.h[:, c0:512], Vp[:, kb, :], ptb[:, c0:512], kb == 0, kb == nk - 1, [ptb], [po],
                   sig=(kb == nk - 1))
                it += 1
                if h > 0 and kb == min(5, nk - 1):
                    finish_head(h - 1)
                if m == 0 and kb % 3 == 2:
                    wepi_consume()
        finish_head(3)
        if m == 0:
            while wepi_state["done"] < len(wepi_jobs):
                wepi_consume()
        cp("dve", rsum[:].rearrange("p a b -> p (a b)"), bk[6].h[:, 0:16], [bk[6]], [rsum])
        recip(rsum[:], rsum[:], [rsum], [rsum])

        if limit == "att":
            return finish()
        AR.release(mPT)
        o1 = A("o1", [128, 512], F32)
        sg1 = A("sg1", [128, 1], F32)
        sg2 = A("sg2", [128, 1], F32)
        sjk = A("sjk", [128, 1024], BF16)
        yb = A("yb", [128, 512], BF16)
        yaT = A("yaT", [128, 4, 128], BF16)
        xrs = [A("xr0", [128, 1024], F32), A("xr1", [128, 1024], F32)]
        x1s = [A("x10", [128, 1024], F32), A("x11", [128, 1024], F32)]
        x1b = A("x1b", [128, 1024], BF16)
        x1T = A("x1T", [128, 8, 128], BF16)
        gt = A("gt", [128, 1024], F32)
        pfs = [A("pf0", [128, 256], F32), A("pf1", [128, 256], F32)]
        pb16 = A("pb16", [128, 256], BF16)
        pTt = A("pTt", [128, 2, 128], BF16)

        def epi_tile(s):
            t = 4 * m + s
            xr, x1, pf = xrs[s % 2], x1s[s % 2], pfs[s % 2]
            S.dma(lambda e: e.dma_start(out=xr[:], in_=D["x_own"][t]), sl_xr[s % 2], (), [xr])
            S.dma(lambda e: e.dma_start(out=pf[:], in_=D["p_own"][t]), sl_p[s % 2], (), [pf])
            yield
            pov = bk[6]
            for h in range(4):
                mm(pov.h[:, h * 128:(h + 1) * 128], OTb[:, h, s * 128:(s + 1) * 128], w_uv[:, h, :], True, True,
                   [OTb, w_uv], [pov], sig=(h == 3))
            yield
            tt("dve", o1[:].rearrange("p (a b) -> p a b", a=4), pov.h[:, :].rearrange("p (a b) -> p a b", a=4),
               bc(rsum[:, s, :], 2, 128), ALU.mult, [pov, rsum], [o1])
            tt("dve", o1[:], o1[:], sza[:, s, :], ALU.mult, [o1, sza], [o1])
            yield
            act(sjk[:, 0:512], o1[:], AF.Square, [o1], [sjk, sg1], accum=sg1[:])
            rstd_from(sg1[:], sg1[:], 512, [sg1], [sg1])
            yield
            ts("dve", yb[:], o1[:], sg1[:], None, ALU.mult, None, [o1, sg1], [yb])
            yield
            pty = bkb(7).rearrange("p (c t) -> p c t", c=8)
            for c in range(4):
                tr(pty[:, c, :], yb[:, c * 128:(c + 1) * 128], idb[:], [yb, idb], [bk[7]], sig=(c == 3))
            cp("dve", yaT[:], pty[:, 0:4, :], [bk[7]], [yaT])
            yield
            if WARM:
                pe_warm(bk[0], w_o[:, 0, 0:512], w_o, WARM)
            for n in range(2):
                pz = bk[n]
                for c in range(8):
                    lhs = yaT[:, c, :] if c < 4 else ycT[:, c - 4, s * 128:(s + 1) * 128]
                    mm(pz.h[:, :], lhs, w_o[:, c, n * 512:(n + 1) * 512], c == 0, c == 7, [yaT, ycT, w_o], [pz],
                       sig=(c == 7))
                tt("dve", x1[:, n * 512:(n + 1) * 512], pz.h[:, :], xr[:, n * 512:(n + 1) * 512], ALU.add,
                   [pz, xr], [x1])
                yield
            act(sjk[:], x1[:], AF.Square, [x1], [sjk, sg2], accum=sg2[:])
            rstd_from(sg2[:], sg2[:], 1024, [sg2], [sg2])
            yield
            ts("dve", x1b[:], x1[:], sg2[:], None, ALU.mult, None, [x1, sg2], [x1b])
            cp("dve", pb16[:], pf[:], [pf], [pb16])
            yield
            ptx = bkb(2).rearrange("p (c t) -> p c t", c=8)
            for c in range(8):
                tr(ptx[:, c, :], x1b[:, c * 128:(c + 1) * 128], idb[:], [x1b, idb], [bk[2]], sig=(c == 7))
            cp("dve", x1T[:], ptx, [bk[2]], [x1T])
            yield
            ptp = bkb(5).rearrange("p (c t) -> p c t", c=8)
            for c in range(2):
                tr(ptp[:, 4 + c, :], pb16[:, c * 128:(c + 1) * 128], idb[:], [pb16, idb], [bk[5]], sig=(c == 1))
            cp("dve", pTt[:], ptp[:, 4:6, :], [bk[5]], [pTt])
            yield
            ost = outst[t % 2]
            if WARM:
                pe_warm(bk[3], w_o[:, 0, 0:512], w_o, WARM)
            for n in range(2):
                pg = bk[3 + n]
                for c in range(8):
                    mm(pg.h[:, :], x1T[:, c, :], w_plg[:, c, n * 512:(n + 1) * 512], c == 0, c == 7, [x1T, w_plg],
                       [pg], sig=(c == 7))
                gts = gt[:, n * 512:(n + 1) * 512]
                sigmoid_act(gts, pg.h[:, :], [pg], [gt])
                yield
                pp = bk[5]
                for c in range(2):
                    mm(pp.h[:, :], pTt[:, c, :], w_pl[:, c, n * 512:(n + 1) * 512], c == 0, c == 1, [pTt, w_pl], [pp],
                       sig=(c == 1))
                yield
                tt("dve", gts, pp.h[:, :], gts, ALU.mult, [pp, gt], [gt])
                tt("dve", ost[:, n * 512:(n + 1) * 512], gts, x1[:, n * 512:(n + 1) * 512], ALU.add, [gt, x1], [ost])
                yield
            tok = S.dma(lambda e: e.dma_start(out=out[t], in_=ost[:]), sl_out[t % 2], [ost], ())
            out_toks.append(tok)

        run_pipelined([epi_tile(s) for s in range(4)], skew=8, maxlive=2)
        AR.release(mAt)

    S.wait_all("sp", out_toks[-2:])
    S.emit()
    return nc, AR.peak


_CACHE = {}


def _get_program():
    if "nc" not in _CACHE:
        _CACHE["nc"] = build_program()[0]
    return _CACHE["nc"]


def _host_inputs(x, p, positions, g_in, w_in, g_cq, w_uq, g_ckv, w_ukv, g_q, g_k, conv_w, g_oa, g_oc, w_o, w_pl,
                 w_plg, g_pl):
    f = lambda a: np.ascontiguousarray(np.asarray(a, dtype=np.float32))
    x = f(x)
    p = f(p)
    positions = np.ascontiguousarray(np.asarray(positions, dtype=np.int32))
    inv_freq = (1.0 / (10000.0 ** (np.arange(0, 64, 2, dtype=np.float32) / np.float32(64)))).astype(np.float32)
    tri = np.where(np.arange(128)[:, None] <= np.arange(128)[None, :], 0.0, NEG).astype(np.float32)
    full = np.full((128, 128), NEG, np.float32)
    zero = np.zeros((128, 128), np.float32)
    shared = dict(
        ident=np.eye(128, dtype=np.float32),
        inv_freq=inv_freq,
        g_in_pp=f(f(g_in)[0].reshape(8, 128).T),
        g_pl_pp=f(f(g_pl)[0].reshape(8, 128).T),
        g_o_pp=f(np.concatenate([f(g_oa)[0], f(g_oc)[0]]).reshape(8, 128).T),
        g_cq_pp=f(f(g_cq)[0].reshape(2, 128).T),
        conv_wT=f(f(conv_w)[0].reshape(3, 4, 128).transpose(2, 1, 0)),
        w_in=f(w_in)[0], w_uq=f(w_uq)[0], g_ckv=f(g_ckv)[0], w_ukv=f(w_ukv)[0], g_q=f(g_q)[0], g_k=f(g_k)[0],
        w_o=f(w_o)[0], w_pl=f(w_pl)[0], w_plg=f(w_plg)[0],
    )
    in_maps, blocks_all = [], []
    for core in range(8):
        b, j = core // 4, core % 4
        blocks = [16 * m + o for m in range(4) for o in (j, 7 - j, 8 + j, 15 - j)]
        blocks_all.append(blocks)
        xb_ = x[b].reshape(64, 128, 1024)
        halo = np.zeros((16, 2, 1024), np.float32)
        for i, blk in enumerate(blocks):
            if blk > 0:
                halo[i] = x[b, blk * 128 - 2:blk * 128]
        pos_b = positions[b].reshape(64, 128)

        def mk(r, target):
            return zero if r < target else (tri if r == target else full)
        masks = np.stack([mk(r, j) for r in range(4)] + [mk(r, 3 - j) for r in range(4)], axis=1)
        d = dict(shared)
        d.update(
            x_kv=x[b],
            x_own=f(xb_[blocks]),
            x_halo=f(halo.reshape(32, 1024)),
            p_own=f(p[0, b].reshape(64, 128, 256)[blocks]),
            pos_kv=np.ascontiguousarray(pos_b.T),
            pos_own=np.ascontiguousarray(pos_b[blocks].T),
            masks=f(masks),
        )
        in_maps.append(d)
    return in_maps, blocks_all


def kernel(**inputs):
    in_maps, blocks_all = _host_inputs(**inputs)
    nc = _get_program()
    res = run_bass_kernel_spmd(nc, in_maps, core_ids=list(range(8)))
    out = np.empty((2, 64, 128, 1024), np.float32)
    for core in range(8):
        o = np.asarray(res.results[core]["out_own"]).reshape(16, 128, 1024)
        out[core // 4, blocks_all[core]] = o
    return out.reshape(2, 8192, 1024)
```
